# Optimizing a Trainium2 kernel written in Bass

```python
import math
import jax, jax.numpy as jnp
from jax import lax
import numpy as np

D_MODEL = 2048
BATCH = 4
SEQ = 2048
DEPTH = 1
DEC_BATCH = 32
DEC_SEQ = 64
PAST_LEN = 2048

CHUNK = 64
D_RNN = 1024
LRU_BLOCKS = 16
LRU_BLOCK = D_RNN // LRU_BLOCKS
CONV_W = 4
LRU_C = 8.0
RET_HEADS = 8
RET_DK = 128
RET_DV = 128
D_RET_K = RET_HEADS * RET_DK
D_RET_V = RET_HEADS * RET_DV
D_FF = 5632
DN_ALPHA = (2.0 * DEPTH) ** 0.25
DN_BETA = (8.0 * DEPTH) ** -0.25
LN_EPS = 1e-5
GN_EPS = 1e-5
ROPE_BASE = 10000.0
D_IN = 2 * D_RNN + 2 * D_RET_K + 2 * D_RET_V + 2 * D_MODEL
SPLIT_POINTS = (D_RNN, 2 * D_RNN, 2 * D_RNN + D_RET_K, 2 * D_RNN + 2 * D_RET_K,
                2 * D_RNN + 2 * D_RET_K + D_RET_V, 2 * D_RNN + 2 * D_RET_K + 2 * D_RET_V,
                2 * D_RNN + 2 * D_RET_K + 2 * D_RET_V + D_MODEL)

kernel_name = 'hawk_retnet_macaron_deepnorm_stream_step'


def layer_norm(x, g, b):
    xf = x.astype(jnp.float32)
    mu = jnp.mean(xf, -1, keepdims=True)
    var = jnp.mean(jnp.square(xf - mu), -1, keepdims=True)
    return ((xf - mu) * lax.rsqrt(var + LN_EPS) * g.astype(jnp.float32) + b.astype(jnp.float32)).astype(x.dtype)


def swiglu_ffn(x, w_gate, w_up, w_down):
    return (jax.nn.silu(x @ w_gate) * (x @ w_up)) @ w_down


def causal_dwconv(xb, conv_state, w, b):
    T = xb.shape[1]
    xpad = jnp.concatenate([conv_state.astype(xb.dtype), xb], axis=1)
    y = b
    for j in range(CONV_W):
        y = y + w[j] * xpad[:, j:j + T]
    return y, xpad[:, -(CONV_W - 1):]


def rg_lru(x, h0, rg_w, rg_b, ig_w, ig_b, lam):
    B, T, _ = x.shape
    xf = x.astype(jnp.float32)
    xg = xf.reshape(B, T, LRU_BLOCKS, LRU_BLOCK)
    r = jax.nn.sigmoid(jnp.einsum('btnk,nkj->btnj', xg, rg_w.astype(jnp.float32)).reshape(B, T, D_RNN) + rg_b)
    i = jax.nn.sigmoid(jnp.einsum('btnk,nkj->btnj', xg, ig_w.astype(jnp.float32)).reshape(B, T, D_RNN) + ig_b)
    log_a = -LRU_C * r * jax.nn.softplus(-lam.astype(jnp.float32))
    a = jnp.exp(log_a)
    u = jnp.sqrt(-jnp.expm1(2.0 * log_a)) * (i * xf)

    def combine(left, right):
        a1, b1 = left
        a2, b2 = right
        return a1 * a2, a2 * b1 + b2

    A, Bc = lax.associative_scan(combine, (a, u), axis=1)
    h = A * h0.astype(jnp.float32)[:, None] + Bc
    return h, h[:, -1]


def rotary(x, pos):
    d = x.shape[-1]
    inv_freq = ROPE_BASE ** (-jnp.arange(0, d, 2, dtype=jnp.float32) / d)
    ang = pos.astype(jnp.float32)[:, None] * inv_freq[None, :]
    cos = jnp.cos(ang)[None, :, None, :]
    sin = jnp.sin(ang)[None, :, None, :]
    x1, x2 = x[..., : d // 2], x[..., d // 2:]
    return jnp.concatenate([x1 * cos - x2 * sin, x1 * sin + x2 * cos], axis=-1)


def retention_chunkwise(q, k, v, S0):
    B, T, H, DK = q.shape
    DV = v.shape[-1]
    C = min(CHUNK, T)
    n = T // C
    log_g = jnp.log1p(-jnp.exp2(-5.0 - jnp.arange(H, dtype=jnp.float32)))
    idx = jnp.arange(C, dtype=jnp.float32)
    diff = idx[:, None] - idx[None, :]
    dmask = jnp.where(diff >= 0, jnp.exp(log_g[:, None, None] * jnp.maximum(diff, 0.0)), 0.0)
    q_dec = jnp.exp(log_g[:, None] * (idx[None, :] + 1.0)).T[None, :, :, None]
    k_dec = jnp.exp(log_g[:, None] * (C - 1.0 - idx[None, :])).T[None, :, :, None]
    chunk_dec = jnp.exp(log_g * C)[None, :, None, None]

    def to_chunks(t):
        return jnp.moveaxis(t.reshape(B, n, C, H, t.shape[-1]), 1, 0)

    def step(S, inp):
        qi, ki, vi = inp
        s = jnp.einsum('bihd,bjhd->bhij', qi, ki) * dmask
        o = jnp.einsum('bhij,bjhe->bihe', s, vi) + jnp.einsum('bihd,bhde->bihe', qi * q_dec, S)
        S = chunk_dec * S + jnp.einsum('bjhd,bjhe->bhde', ki * k_dec, vi)
        return S, o

    S, o = lax.scan(step, S0, (to_chunks(q), to_chunks(k), to_chunks(v)))
    return jnp.moveaxis(o, 0, 1).reshape(B, T, H, DV), S


def trunk_layer(x, pos, conv0, h0, S0,
                f1_g, f1_u, f1_d, ln1_g, ln1_b,
                w_in, conv_w, conv_b, rg_w, rg_b, ig_w, ig_b, lam,
                gn_g, gn_b, w_a_proj, w_b_proj, w_o, ln2_g, ln2_b,
                f2_g, f2_u, f2_d, ln3_g, ln3_b):
    B, T, _ = x.shape
    x = layer_norm(DN_ALPHA * x + 0.5 * swiglu_ffn(x, f1_g, f1_u, f1_d), ln1_g, ln1_b)

    z = x @ w_in
    xa, ga, q, k, v, g, gate_a, gate_b = jnp.split(z, list(SPLIT_POINTS), axis=-1)

    xc, conv_new = causal_dwconv(xa, conv0, conv_w, conv_b)
    h, h_last = rg_lru(xc, h0, rg_w, rg_b, ig_w, ig_b, lam)
    ya = (h * jax.nn.gelu(ga.astype(jnp.float32))).astype(x.dtype)

    qh = rotary(q.reshape(B, T, RET_HEADS, RET_DK).astype(jnp.float32), pos)
    kh = rotary(k.reshape(B, T, RET_HEADS, RET_DK).astype(jnp.float32), pos) * (RET_DK ** -0.5)
    vh = v.reshape(B, T, RET_HEADS, RET_DV).astype(jnp.float32)
    o, S_new = retention_chunkwise(qh, kh, vh, S0.astype(jnp.float32))
    mu = jnp.mean(o, -1, keepdims=True)
    var = jnp.mean(jnp.square(o - mu), -1, keepdims=True)
    o = ((o - mu) * lax.rsqrt(var + GN_EPS)).reshape(B, T, D_RET_V) * gn_g.astype(jnp.float32) + gn_b.astype(jnp.float32)
    yb = (jax.nn.silu(g.astype(jnp.float32)) * o).astype(x.dtype)

    merged = jax.nn.sigmoid(gate_a) * (ya @ w_a_proj) + jax.nn.sigmoid(gate_b) * (yb @ w_b_proj)
    x = layer_norm(DN_ALPHA * x + merged @ w_o, ln2_g, ln2_b)

    x = layer_norm(DN_ALPHA * x + 0.5 * swiglu_ffn(x, f2_g, f2_u, f2_d), ln3_g, ln3_b)
    return x, conv_new, h_last, S_new


def setup_inputs(seed: int = 0) -> dict:
    key = jax.random.key(seed)
    ks = jax.random.split(key, 32)
    f32 = jnp.float32
    L = DEPTH

    def nrm(k, shape, scale):
        return jax.random.normal(k, shape, f32) * scale

    u = jax.random.uniform(ks[13], (L, D_RNN), f32, 0.9, 0.999)
    s = u ** (1.0 / LRU_C)
    lru_lambda = jnp.log(s) - jnp.log1p(-s)
    return {
        'x_prompt': nrm(ks[0], (BATCH, SEQ, D_MODEL), 1.0),
        'x_sample': nrm(ks[1], (DEC_BATCH, DEC_SEQ, D_MODEL), 1.0),
        'state_conv': nrm(ks[2], (L, DEC_BATCH, CONV_W - 1, D_RNN), 1.0),
        'state_lru': nrm(ks[3], (L, DEC_BATCH, D_RNN), 0.5),
        'state_ret': nrm(ks[4], (L, DEC_BATCH, RET_HEADS, RET_DK, RET_DV), 0.5),
        'ffn1_w_gate': nrm(ks[5], (L, D_MODEL, D_FF), D_MODEL ** -0.5),
        'ffn1_w_up': nrm(ks[6], (L, D_MODEL, D_FF), D_MODEL ** -0.5),
        'ffn1_w_down': nrm(ks[7], (L, D_FF, D_MODEL), DN_BETA * D_FF ** -0.5),
        'ln1_g': 1.0 + nrm(ks[8], (L, D_MODEL), 0.02),
        'ln1_b': nrm(ks[9], (L, D_MODEL), 0.02),
        'w_in': nrm(ks[10], (L, D_MODEL, D_IN), D_MODEL ** -0.5),
        'conv_w': nrm(ks[11], (L, CONV_W, D_RNN), CONV_W ** -0.5),
        'conv_b': nrm(ks[12], (L, D_RNN), 0.02),
        'rg_w': nrm(ks[14], (L, LRU_BLOCKS, LRU_BLOCK, LRU_BLOCK), LRU_BLOCK ** -0.5),
        'rg_b': nrm(ks[15], (L, D_RNN), 0.02),
        'ig_w': nrm(ks[16], (L, LRU_BLOCKS, LRU_BLOCK, LRU_BLOCK), LRU_BLOCK ** -0.5),
        'ig_b': nrm(ks[17], (L, D_RNN), 0.02),
        'lru_lambda': lru_lambda,
        'ret_gn_g': 1.0 + nrm(ks[18], (L, D_RET_V), 0.02),
        'ret_gn_b': nrm(ks[19], (L, D_RET_V), 0.02),
        'w_a_proj': nrm(ks[20], (L, D_RNN, D_MODEL), D_RNN ** -0.5),
        'w_b_proj': nrm(ks[21], (L, D_RET_V, D_MODEL), D_RET_V ** -0.5),
        'w_o': nrm(ks[22], (L, D_MODEL, D_MODEL), DN_BETA * D_MODEL ** -0.5),
        'ln2_g': 1.0 + nrm(ks[23], (L, D_MODEL), 0.02),
        'ln2_b': nrm(ks[24], (L, D_MODEL), 0.02),
        'ffn2_w_gate': nrm(ks[25], (L, D_MODEL, D_FF), D_MODEL ** -0.5),
        'ffn2_w_up': nrm(ks[26], (L, D_MODEL, D_FF), D_MODEL ** -0.5),
        'ffn2_w_down': nrm(ks[27], (L, D_FF, D_MODEL), DN_BETA * D_FF ** -0.5),
        'ln3_g': 1.0 + nrm(ks[28], (L, D_MODEL), 0.02),
        'ln3_b': nrm(ks[29], (L, D_MODEL), 0.02),
    }


def reference(x_prompt, x_sample, state_conv, state_lru, state_ret,
              ffn1_w_gate, ffn1_w_up, ffn1_w_down, ln1_g, ln1_b,
              w_in, conv_w, conv_b, rg_w, rg_b, ig_w, ig_b, lru_lambda,
              ret_gn_g, ret_gn_b, w_a_proj, w_b_proj, w_o, ln2_g, ln2_b,
              ffn2_w_gate, ffn2_w_up, ffn2_w_down, ln3_g, ln3_b):

    def run(x, pos, conv0, h0, S0):
        convs, hs, Ss = [], [], []
        for l in range(DEPTH):
            x, c, h, S = trunk_layer(
                x, pos, conv0[l], h0[l], S0[l],
                ffn1_w_gate[l], ffn1_w_up[l], ffn1_w_down[l], ln1_g[l], ln1_b[l],
                w_in[l], conv_w[l], conv_b[l], rg_w[l], rg_b[l], ig_w[l], ig_b[l], lru_lambda[l],
                ret_gn_g[l], ret_gn_b[l], w_a_proj[l], w_b_proj[l], w_o[l], ln2_g[l], ln2_b[l],
                ffn2_w_gate[l], ffn2_w_up[l], ffn2_w_down[l], ln3_g[l], ln3_b[l])
            convs.append(c.astype(state_conv.dtype))
            hs.append(h.astype(state_lru.dtype))
            Ss.append(S.astype(state_ret.dtype))
        return x, jnp.stack(convs), jnp.stack(hs), jnp.stack(Ss)

    Bp, Tp, _ = x_prompt.shape
    pos_p = jnp.arange(Tp, dtype=jnp.int32)
    conv0_p = jnp.zeros((DEPTH, Bp, CONV_W - 1, D_RNN), state_conv.dtype)
    h0_p = jnp.zeros((DEPTH, Bp, D_RNN), state_lru.dtype)
    S0_p = jnp.zeros((DEPTH, Bp, RET_HEADS, RET_DK, RET_DV), state_ret.dtype)
    y_prompt, conv_prompt, lru_prompt, ret_prompt = run(x_prompt, pos_p, conv0_p, h0_p, S0_p)

    Ts = x_sample.shape[1]
    pos_s = PAST_LEN + jnp.arange(Ts, dtype=jnp.int32)
    y_sample, conv_sample, lru_sample, ret_sample = run(x_sample, pos_s, state_conv, state_lru, state_ret)

    return (y_prompt, y_sample, conv_prompt, lru_prompt, ret_prompt, conv_sample, lru_sample, ret_sample)
```

```python
import math
from contextlib import ExitStack

import numpy as np
import concourse.bass as bass
import concourse.mybir as mybir
from concourse.bass_utils import run_bass_kernel_spmd

F32 = mybir.dt.float32
BF16 = mybir.dt.bfloat16
ALU = mybir.AluOpType
AF = mybir.ActivationFunctionType

D = 2048
DFF = 5632
DIN = 10240
NDC = 16
NG = 22
TM = 1280
TP = 1024
TILES_M = [(0, 512), (512, 512), (1024, 256)]
TILES_P = [(0, 512), (512, 512)]
ALPHA = 2.0 ** 0.25
LN_EPS = 1e-5
H = 8
RSZ = 6656
SELF_WAIT = {"pe": False, "act": True, "dve": True, "pool": True, "sp": False}
ENGS = ["pe", "act", "dve", "pool", "sp"]
LOGG = [math.log1p(-2.0 ** (-5.0 - h)) for h in range(H)]
G64 = [math.exp(LOGG[h] * 64.0) for h in range(H)]
NV = 176
V_LN = 0
V_CW = 96
V_CB = 128
V_RGB = 136
V_IGB = 144
V_LAM = 152
V_GNG = 160
V_GNB = 168


import os
LIMIT = float(os.environ.get("KLIMIT", "999"))


class _Stop(Exception):
    pass


class Prog:
    def cp(self, n):
        if n > LIMIT:
            raise _Stop()

    def __init__(self):
        self.ops = {e: [] for e in ENGS}
        self.cnt = {e: 0 for e in ENGS}
        self.known = {e: {} for e in ENGS}
        self.wr = {}
        self.rd = {}
        self.dsem = {}

    def _deps(self, eng, reads, writes):
        deps = {}

        def add(tok):
            s, v = tok
            if s == "E" + eng and not SELF_WAIT[eng]:
                return
            if deps.get(s, 0) < v:
                deps[s] = v

        for k in reads:
            if k in self.wr:
                add(self.wr[k])
        for k in writes:
            if k in self.wr:
                add(self.wr[k])
            for tok in self.rd.get(k, {}).items():
                add(tok)
        kn = self.known[eng]
        waits = []
        for s, v in deps.items():
            if kn.get(s, 0) < v:
                kn[s] = v
                waits.append((s, v))
        return waits

    def _commit(self, reads, writes, tok):
        s, v = tok
        for k in reads:
            d = self.rd.setdefault(k, {})
            if d.get(s, 0) < v:
                d[s] = v
        for k in writes:
            self.wr[k] = tok
            self.rd[k] = {}

    cap = None

    def op(self, eng, fn, reads=(), writes=()):
        if self.cap is not None:
            self.cap.append((eng, fn, reads, writes))
            return
        waits = self._deps(eng, reads, writes)
        self.cnt[eng] += 1
        tok = ("E" + eng, self.cnt[eng])
        self.ops[eng].append((waits, fn, ("E" + eng, 1)))
        self._commit(reads, writes, tok)

    def dma(self, q, fns, sem, reads=(), writes=()):
        waits = self._deps(q, reads, writes)
        for i, fn in enumerate(fns):
            self.ops[q].append((waits if i == 0 else [], fn, ("D" + sem, 16)))
        self.dsem[sem] = self.dsem.get(sem, 0) + 16 * len(fns)
        self._commit(reads, writes, ("D" + sem, self.dsem[sem]))

    def barrier(self, keep_prefix=("slab",), skip=("pool",)):
        toks = {"E" + e: self.cnt[e] for e in ENGS if self.cnt[e] > 0}
        for s, v in self.dsem.items():
            if not s.startswith(keep_prefix):
                toks["D" + s] = v
        for e in ENGS:
            if e in skip:
                continue
            kn = self.known[e]
            waits = []
            for s, v in toks.items():
                if s == "E" + e and not SELF_WAIT[e]:
                    continue
                if kn.get(s, 0) < v:
                    kn[s] = v
                    waits.append((s, v))
            if waits:
                self.ops[e].append((waits, None, None))
        self.wr = {k: v for k, v in self.wr.items() if isinstance(k, tuple) and str(k[0]).startswith(keep_prefix)}
        self.rd = {k: v for k, v in self.rd.items() if isinstance(k, tuple) and str(k[0]).startswith(keep_prefix)}


def build_nc():
    nc = bass.Bass("TRN2", target_bir_lowering=False)
    P = Prog()

    def din(name, shape):
        return nc.dram_tensor(name, list(shape), F32, kind="ExternalInput").ap()

    def dout(name, shape):
        return nc.dram_tensor(name, list(shape), F32, kind="ExternalOutput").ap()

    xmT = din("xmT", [D, TM])
    xpT = din("xpT", [D, TP])
    maskd = din("mask", [128, 1])
    csd = din("cs", [128, 2, TM + TP])
    stcT = din("stcT", [128, 8 * 4 * 3])
    stlT = din("stlT", [128, 8 * 4])
    strt = din("strt", [4, H, 128, 128])
    vecd = din("vecT", [128, NV])
    bdd = din("bd", [128, 16 * 128])
    identd = din("ident", [128, 128])
    permd = din("perm", [128, 128])
    dmTd = din("dmT", [128, H * 128])
    qdecd = din("qdec", [128, H * 64])
    kdecd = din("kdecP", [128, H])
    W = {}
    for nm, shp in [("f1g", [D, DFF]), ("f1u", [D, DFF]), ("f1d", [DFF, D]), ("w_in", [D, DIN]),
                    ("w_a", [1024, D]), ("w_b", [1024, D]), ("w_o", [D, D]),
                    ("f2g", [D, DFF]), ("f2u", [D, DFF]), ("f2d", [DFF, D])]:
        W[nm] = din(nm, shp)
    yT = dout("yT", [D, TM])
    hoT = dout("hoT", [128, 8 * 5])
    coT = dout("coT", [128, 8 * 5 * 3])
    S_o = dout("S_o", [5, H, 128, 128])
    DEBUG = bool(os.environ.get("KDEBUG"))
    if DEBUG:
        dbg = dout("dbg", [128, 64, 512])

    es = ExitStack()
    uid = [0]

    def sbt(name, shape, dt=F32):
        uid[0] += 1
        return nc.sbuf_tensor("s%d_%s" % (uid[0], name), list(shape), dt)

    def sb(name, shape, dt=F32):
        return es.enter_context(sbt(name, shape, dt))

    acc = sb("acc", [128, NDC, TM])
    vec = sb("vec", [128, NV])
    lnsc = sb("lnsc", [128, 6 * 16])
    bd = sb("bd", [128, 16, 128], BF16)
    ident = sb("ident", [128, 128], BF16)
    perm = sb("perm", [128, 128], BF16)
    ones = sb("ones", [128, 128])
    onesb = sb("onesb", [128, 128], BF16)
    dmT = sb("dmT", [128, H, 128])
    qdec = sb("qdec", [128, H, 64])
    kdecP = sb("kdecP", [128, H])
    c8 = sb("c8", [128, 16])
    ctmp = sb("ctmp", [128, 32])
    negb = sb("negb", [128, 16])
    mask = sb("mask", [128, 1])
    hst = sb("hst", [128, 8, 5])
    cst = sb("cst", [128, 8, 5, 3])
    Sw = [sb("Sw0", [128, H, 128])]
    ps_all = es.enter_context(nc.psum_tensor("psum_all", [128, 8 * 512], F32))

    if DEBUG:
        dstg = sb("dstg", [128, 512])

    def dump(idx, src, keys):
        if not DEBUG:
            return
        P.op("dve", lambda e: e.tensor_copy(dstg[:], src), reads=tuple(keys), writes=("dstg",))
        P.dma("sp", [lambda e: e.dma_start(out=dbg[:, idx, :], in_=dstg[:])], "dbg", reads=("dstg",))

    psi = [0]

    def psum():
        i = psi[0] % 8
        psi[0] += 1
        return ps_all[:, i * 512:(i + 1) * 512], ("ps", i)

    slabs = {"t": [], "i": 0}

    def load_slab(src3, nk, ncol):
        n = len(slabs["t"])
        i = slabs["i"] % n
        slabs["i"] += 1
        t = slabs["t"][i]
        view = t[:, 0:nk * ncol].rearrange("p (c f) -> p c f", f=ncol)
        key = ("slab", i)
        nrow = 128 * nk
        nsplit = max(1, nrow // 1024)
        step = nk // nsplit
        fns = []
        for s in range(nsplit):
            fns.append(lambda e, s=s: e.dma_start(out=view[:, s * step:(s + 1) * step, :],
                                                  in_=src3[:, s * step:(s + 1) * step, :]))
        P.dma("pool", fns, "slab%d" % i, reads=(), writes=(key,))
        return view, key

    def wcols(w, c0, ncol):
        return w.rearrange("(c p) f -> p c f", p=128)[:, :, c0:c0 + ncol]

    def wrows(w, r0, nrow):
        return w[r0:r0 + nrow, :].rearrange("(c p) f -> p c f", p=128)

    def ld(dst, src, sem, key, q="sp"):
        P.dma(q, [lambda e: e.dma_start(out=dst, in_=src)], sem, writes=(key,))

    ld(vec[:], vecd[:, :], "c0", "vec")
    ld(dmT[:].rearrange("p h f -> p (h f)"), dmTd[:, :], "c1", "dmT")
    ld(qdec[:].rearrange("p h f -> p (h f)"), qdecd[:, :], "c2", "qdec")
    ld(kdecP[:], kdecd[:, :], "c3", "kdecP")
    ld(mask[:], maskd[:, :], "c4", "mask")
    ld(bd[:].rearrange("p h f -> p (h f)"), bdd[:, :], "c5", "bd", q="pool")
    ld(ident[:], identd[:, :], "c6", "ident", q="pool")
    ld(perm[:], permd[:, :], "c7", "perm", q="pool")
    P.op("dve", lambda e: e.memset(ones[:], 1.0), writes=("ones",))
    P.op("dve", lambda e: e.tensor_copy(onesb[:], ones[:]), reads=("ones",), writes=("onesb",))
    P.op("dve", lambda e: e.memset(Sw[0][:], 0.0), writes=(("Sw", 0),))
    P.op("dve", lambda e: e.memset(hst[:], 0.0), writes=("hst",))
    P.op("dve", lambda e: e.memset(cst[:], 0.0), writes=("cst",))
    P.op("dve", lambda e: e.tensor_scalar(lnsc[:], vec[:, 0:96], ALPHA, None, ALU.mult),
         reads=("vec",), writes=("lnsc",))
    lam = vec[:, V_LAM:V_LAM + 8]
    yv, y2, y3, sm_ = ctmp[:, 0:8], ctmp[:, 8:16], ctmp[:, 16:24], ctmp[:, 24:32]
    P.op("act", lambda e: e.activation(yv, lam, AF.Exp, scale=-1.0), reads=("vec",), writes=("ct0",))
    P.op("dve", lambda e: e.tensor_tensor(y2, yv, yv, ALU.mult), reads=("ct0",), writes=("ct1",))
    P.op("dve", lambda e: e.tensor_tensor(y3, y2, yv, ALU.mult), reads=("ct0", "ct1"), writes=("ct2",))
    P.op("dve", lambda e: e.scalar_tensor_tensor(sm_, y2, -0.5, yv, ALU.mult, ALU.add),
         reads=("ct0", "ct1"), writes=("ct3",))
    P.op("dve", lambda e: e.scalar_tensor_tensor(sm_, y3, 1.0 / 3.0, sm_, ALU.mult, ALU.add),
         reads=("ct2", "ct3"), writes=("ct3",))
    P.op("dve", lambda e: e.tensor_tensor(y2, y2, y2, ALU.mult), reads=("ct1",), writes=("ct1",))
    P.op("dve", lambda e: e.scalar_tensor_tensor(sm_, y2, -0.25, sm_, ALU.mult, ALU.add),
         reads=("ct1", "ct3"), writes=("ct3",))
    P.op("dve", lambda e: e.tensor_scalar(negb[:], vec[:, V_RGB:V_RGB + 16], -1.0, None, ALU.mult), reads=("vec",), writes=("negb",))
    P.op("dve", lambda e: e.tensor_scalar(c8[:, 0:8], sm_, -8.0, None, ALU.mult), reads=("ct3",), writes=("c8a",))
    P.op("dve", lambda e: e.tensor_scalar(c8[:, 8:16], sm_, -16.0, None, ALU.mult), reads=("ct3",), writes=("c8b",))

    def load_x(xT, T, xb, dma=True, conv=True):
        for dc in range(NDC):
            if dma:
                P.dma("sp", [lambda e, dc=dc: e.dma_start(out=acc[:, dc, 0:T], in_=xT[dc * 128:(dc + 1) * 128, :])],
                      "xl%d" % dc, writes=(("acc", dc),))
            if not conv:
                continue
            P.op("act", lambda e, dc=dc: e.activation(xb[:, dc, 0:T], acc[:, dc, 0:T], AF.Identity),
                 reads=(("acc", dc),), writes=(("xb", dc),))
            P.op("dve", lambda e, dc=dc: e.tensor_scalar(acc[:, dc, 0:T], acc[:, dc, 0:T], ALPHA, None, ALU.mult),
                 reads=(("acc", dc),), writes=(("acc", dc),))

    def ffn(tiles, wg, wu, wd, xb, hg, sgt):
        def gu(g):
            sg, kg = load_slab(wcols(wg, g * 256, 256), 16, 256)
            su, ku = load_slab(wcols(wu, g * 256, 256), 16, 256)
            sd, kd = load_slab(wrows(wd, g * 256, 256), 2, D)
            blocks = []
            for fc in range(2):
                for ti, (t0, tw) in enumerate(tiles):
                    def blk(fc=fc, ti=ti, t0=t0, tw=tw):
                        pg, kpg = psum()
                        pu, kpu = psum()
                        for (pp, kpp, ss, kss) in ((pg, kpg, sg, kg), (pu, kpu, su, ku)):
                            for kc in range(16):
                                P.op("pe", lambda e, pp=pp, ss=ss, kc=kc: e.matmul(
                                    pp[:, 0:tw], ss[:, kc, fc * 128:(fc + 1) * 128], xb[:, kc, t0:t0 + tw],
                                    start=(kc == 0), stop=(kc == 15)),
                                    reads=(kss, ("xb", kc)), writes=(kpp,))
                        st = sgt[(fc * 3 + ti) % 2]
                        kst = ("sgt", (fc * 3 + ti) % 2)
                        P.op("act", lambda e: e.activation(st[:, 0:tw], pg[:, 0:tw], AF.Silu),
                             reads=(kpg,), writes=(kst,))
                        P.op("dve", lambda e: e.tensor_tensor(
                            hg[g % 2][:, fc, t0:t0 + tw], pu[:, 0:tw], st[:, 0:tw], ALU.mult),
                            reads=(kpu, kst), writes=(("hg", g % 2, fc, ti),))
                    blocks.append(blk)
            return blocks, sd, kd

        def down(g, sd, kd):
            blocks = []
            for dc in range(NDC):
                for ti, (t0, tw) in enumerate(tiles):
                    def blk(dc=dc, ti=ti, t0=t0, tw=tw):
                        pd, kpd = psum()
                        for fc in range(2):
                            P.op("pe", lambda e, fc=fc: e.matmul(
                                pd[:, 0:tw], sd[:, fc, dc * 128:(dc + 1) * 128], hg[g % 2][:, fc, t0:t0 + tw],
                                start=(fc == 0), stop=(fc == 1)),
                                reads=(kd, ("hg", g % 2, fc, ti)), writes=(kpd,))
                        P.op("dve", lambda e: e.scalar_tensor_tensor(
                            acc[:, dc, t0:t0 + tw], pd[:, 0:tw], 0.5, acc[:, dc, t0:t0 + tw], ALU.mult, ALU.add),
                            reads=(kpd, ("acc", dc)), writes=(("acc", dc),))
                    blocks.append(blk)
            return blocks

        prev = None
        for g in range(NG + 1):
            gub, cur = [], None
            if g < NG:
                gub, sd_, kd_ = gu(g)
                cur = (sd_, kd_)
            dnb = down(g - 1, *prev) if prev is not None else []
            if gub:
                r = (len(dnb) + len(gub) - 1) // len(gub) if dnb else 0
                for i, gb in enumerate(gub):
                    gb()
                    for db in dnb[i * r:(i + 1) * r]:
                        db()
                for db in dnb[len(gub) * r:]:
                    db()
            else:
                for db in dnb:
                    db()
            prev = cur

    def layernorm(tiles, li, xb, lnt, final=False, ost=None):
        gcol = V_LN + li * 32
        bcol = gcol + 16
        sq, means, rstds, tt = lnt

        def stats(ti, t0, tw):
            mean, rstd = means[ti], rstds[ti]
            km, kr_ = ("lnmean", ti), ("lnrstd", ti)
            p1, k1 = psum()
            p2, k2 = psum()
            for dc in range(NDC):
                P.op("pe", lambda e, p1=p1, dc=dc: e.matmul(
                    p1[:, 0:tw], ones[:], acc[:, dc, t0:t0 + tw], start=(dc == 0), stop=(dc == 15)),
                    reads=("ones", ("acc", dc)), writes=(k1,))
            for dc in range(NDC):
                s_ = sq[dc % 2]
                P.op("act", lambda e, s_=s_, dc=dc: e.activation(
                    s_[:, 0:tw], acc[:, dc, t0:t0 + tw], AF.Square), reads=(("acc", dc),), writes=(("lnsq", dc % 2),))
                P.op("pe", lambda e, p2=p2, s_=s_, dc=dc: e.matmul(
                    p2[:, 0:tw], onesb[:], s_[:, 0:tw], start=(dc == 0), stop=(dc == 15)),
                    reads=("onesb", ("lnsq", dc % 2)), writes=(k2,))
            P.op("dve", lambda e, p1=p1: e.tensor_scalar(mean[:, 0:tw], p1[:, 0:tw], 1.0 / D, None, ALU.mult),
                 reads=(k1,), writes=(km,))
            P.op("dve", lambda e: e.tensor_tensor(rstd[:, 0:tw], mean[:, 0:tw], mean[:, 0:tw], ALU.mult),
                 reads=(km,), writes=(kr_,))
            P.op("dve", lambda e, p2=p2: e.scalar_tensor_tensor(
                rstd[:, 0:tw], p2[:, 0:tw], 1.0 / D, rstd[:, 0:tw], ALU.mult, ALU.subtract),
                reads=(k2, kr_), writes=(kr_,))
            P.op("dve", lambda e: e.tensor_scalar(rstd[:, 0:tw], rstd[:, 0:tw], 0.0, LN_EPS, ALU.max, ALU.add),
                 reads=(kr_,), writes=(kr_,))
            P.op("act", lambda e: e.activation(rstd[:, 0:tw], rstd[:, 0:tw], AF.Sqrt),
                 reads=(kr_,), writes=(kr_,))
            P.op("dve", lambda e: e.reciprocal(rstd[:, 0:tw], rstd[:, 0:tw]),
                 reads=(kr_,), writes=(kr_,))

        def norm(ti, t0, tw):
            mean, rstd = means[ti], rstds[ti]
            km, kr_ = ("lnmean", ti), ("lnrstd", ti)
            for dc in range(NDC):
                t = tt[dc % 2]
                kt = ("lnt", dc % 2)
                P.op("dve", lambda e, t=t, dc=dc: e.tensor_tensor(
                    t[:, 0:tw], acc[:, dc, t0:t0 + tw], mean[:, 0:tw], ALU.subtract),
                    reads=(("acc", dc), km), writes=(kt,))
                P.op("dve", lambda e, t=t: e.tensor_tensor(t[:, 0:tw], t[:, 0:tw], rstd[:, 0:tw], ALU.mult),
                     reads=(kt, kr_), writes=(kt,))
                if not final:
                    P.op("act", lambda e, t=t, dc=dc: e.activation(
                        xb[:, dc, t0:t0 + tw], t[:, 0:tw], AF.Identity,
                        bias=vec[:, bcol + dc:bcol + dc + 1], scale=vec[:, gcol + dc:gcol + dc + 1]),
                        reads=(kt, "vec"), writes=(("xb", dc),))
                    if dc % 2 == 0:
                        P.op("act", lambda e, t=t, dc=dc: e.activation(
                            acc[:, dc, t0:t0 + tw], t[:, 0:tw], AF.Identity,
                            bias=lnsc[:, li * 32 + 16 + dc:li * 32 + 17 + dc], scale=lnsc[:, li * 32 + dc:li * 32 + dc + 1]),
                            reads=(kt, "lnsc"), writes=(("acc", dc, ti),))
                    else:
                        P.op("dve", lambda e, t=t, dc=dc: e.tensor_scalar(
                            acc[:, dc, t0:t0 + tw], t[:, 0:tw], lnsc[:, li * 32 + dc:li * 32 + dc + 1],
                            lnsc[:, li * 32 + 16 + dc:li * 32 + 17 + dc], ALU.mult, ALU.add),
                            reads=(kt, "lnsc"), writes=(("acc", dc, ti),))
                else:
                    P.op("act", lambda e, t=t, dc=dc: e.activation(
                        acc[:, dc, t0:t0 + tw], t[:, 0:tw], AF.Identity,
                        bias=vec[:, bcol + dc:bcol + dc + 1], scale=vec[:, gcol + dc:gcol + dc + 1]),
                        reads=(kt, "vec"), writes=(("acc", dc, ti),))
                    P.dma("sp", [lambda e, dc=dc: e.dma_start(
                        out=yT[dc * 128:(dc + 1) * 128, t0:t0 + tw], in_=acc[:, dc, t0:t0 + tw])],
                        "yo%d" % (dc % 4), reads=(("acc", dc, ti),))

        n = len(tiles)
        stats(0, *tiles[0])
        for ti in range(n):
            if ti + 1 < n:
                stats(ti + 1, *tiles[ti + 1])
            norm(ti, *tiles[ti])

    def mixer_tile(xs, xkey, t0g, cs_off, tw, nseq, L, sidx0, full, R, cs_t, ya, yb, sw_of_seq, tile_id):
        nblk = tw // 128
        w_in = W["w_in"]
        P.dma("sp", [lambda e: e.dma_start(out=cs_t[:, :, 0:tw], in_=csd[:, :, cs_off:cs_off + tw])],
              "cs", writes=("cs",))

        def rf(off, n):
            return R[:, off:off + n]

        def rb(off_f32, n):
            return R[:, off_f32:off_f32 + (n + 1) // 2].bitcast(BF16)[:, 0:n]

        def proj(slab, kslab, col0, pp, kpp):
            for kc in range(16):
                P.op("pe", lambda e, kc=kc: e.matmul(pp[:, 0:tw], slab[:, kc, col0:col0 + 128], xs(kc),
                                                     start=(kc == 0), stop=(kc == 15)),
                     reads=(kslab, xkey), writes=(kpp,))


        XP, XC, RR, II, AA, A2, UU, HH, GS, T1, SG, XCB = [i * 528 for i in range(12)]
        tws = tw // 2
        if nseq == 1:
            Ls, ns = tws, 1
        else:
            Ls, ns = L, nseq // 2
        WPs = Ls + 3
        K2 = 2.0 * math.sqrt(2.0 / math.pi)
        lru_slabs = {}

        def job(c, sub):
            so = sub * tws
            sidx = sidx0 if nseq == 1 else sidx0 + sub * ns
            bo_ = sub * 264

            def f(off, n=tws):
                return rf(off + bo_, n)

            def f3(off):
                return f(off).rearrange("p (s l) -> p s l", l=Ls)
            xp3 = rf(XP + bo_, ns * WPs).rearrange("p (s l) -> p s l", l=WPs)
            xc3 = f3(XC)
            xcb = rb(XCB + sub * 132, tws)
            k = lambda nm: (nm, sub)
            col0 = (c % 2) * 128
            sxa, kxa = lru_slabs[("xa", c // 2)]
            cw = lambda j: vec[:, V_CW + c * 4 + j:V_CW + c * 4 + j + 1]
            xsl = lambda kc: xs(kc)[:, so:so + tws]
            st = []
            pxa, kpxa = psum()
            S = []
            for kc in range(16):
                S.append(("pe", lambda e, kc=kc: e.matmul(pxa[:, 0:tws], sxa[:, kc, col0:col0 + 128], xsl(kc),
                                                          start=(kc == 0), stop=(kc == 15)), (kxa, xkey), (kpxa,)))
            st.append(S)
            st.append([
                ("act", lambda e: e.activation(xp3[:, :, 3:WPs], pxa[:, 0:tws].rearrange("p (s l) -> p s l", l=Ls), AF.Identity),
                 (kpxa,), (k("xp"),)),
                ("dve", lambda e: e.tensor_copy(xp3[:, :, 0:3], cst[:, c, sidx:sidx + ns, :]), ("cst",), (k("xp"),)),
            ])
            st.append([("act", lambda e: e.activation(xc3, xp3[:, :, 0:Ls], AF.Identity, bias=vec[:, V_CB + c:V_CB + c + 1], scale=cw(0)),
                        (k("xp"), "vec"), (k("xc"),))])
            S = []
            for j in range(1, 4):
                S.append(("dve", lambda e, j=j: e.scalar_tensor_tensor(xc3, xp3[:, :, j:j + Ls], cw(j), xc3, ALU.mult, ALU.add),
                          (k("xp"), k("xc"), "vec"), (k("xc"),)))
            S.append(("dve", lambda e: e.tensor_copy(cst[:, c, sidx:sidx + ns, :], xp3[:, :, Ls:Ls + 3]), (k("xp"),), ("cst",)))
            st.append(S)
            st.append([("act", lambda e: e.activation(xcb, f(XC), AF.Identity), (k("xc"),), (k("xcb"),))])
            pr, kpr = psum()
            pi, kpi = psum()
            S = [("pe", lambda e: e.matmul(pr[:, 0:tws], bd[:, 2 * c, :], xcb, start=True, stop=True), ("bd", k("xcb")), (kpr,)),
                 ("pe", lambda e: e.matmul(pi[:, 0:tws], bd[:, 2 * c + 1, :], xcb, start=True, stop=True), ("bd", k("xcb")), (kpi,))]
            if full:
                sga, kga = lru_slabs[("ga", c // 2)]
                pga, kpga = psum()
                for kc in range(16):
                    S.append(("pe", lambda e, kc=kc: e.matmul(pga[:, 0:tws], sga[:, kc, col0:col0 + 128], xsl(kc),
                                                              start=(kc == 0), stop=(kc == 15)), (kga, xkey), (kpga,)))
            st.append(S)
            gl = ((pr, kpr, RR, "rr", 0), (pi, kpi, II, "ii", 8))
            S = []
            for (pz, kpz, OFF, kk, bo) in gl:
                S.append(("act", lambda e, pz=pz, OFF=OFF, bo=bo: e.activation(f(OFF), pz[:, 0:tws], AF.Exp,
                                                                              bias=negb[:, bo + c:bo + c + 1], scale=-1.0),
                          (kpz, "negb"), (k(kk),)))
            for (pz, kpz, OFF, kk, bo) in gl:
                S.append(("act", lambda e, OFF=OFF: e.activation(f(OFF), f(OFF), AF.Ln, bias=ones[:, 0:1]), (k(kk), "ones"), (k(kk),)))
            for (pz, kpz, OFF, kk, bo) in gl:
                S.append(("act", lambda e, OFF=OFF: e.activation(f(OFF), f(OFF), AF.Exp, scale=-1.0), (k(kk),), (k(kk),)))
            S.append(("act", lambda e: e.activation(f(AA), f(RR), AF.Exp, scale=c8[:, c:c + 1]), (k("rr"), "c8a"), (k("aa"),)))
            S.append(("act", lambda e: e.activation(f(A2), f(RR), AF.Identity, scale=c8[:, 8 + c:9 + c]), (k("rr"), "c8b"), (k("a2"),)))
            S.append(("act", lambda e: e.activation(f(UU), f(A2), AF.Identity, scale=0.2), (k("a2"),), (k("uu"),)))
            if full:
                S.append(("act", lambda e: e.activation(f(GS), pga[:, 0:tws], AF.Identity), (kpga,), (k("gs"),)))
                S.append(("act", lambda e: e.activation(f(T1), pga[:, 0:tws], AF.Square, scale=math.sqrt(0.044715)), (kpga,), (k("t1"),)))
            st.append(S)
            S = []
            for cst_ in (1.0, 4.0, 12.0, 24.0):
                S.append(("dve", lambda e, cst_=cst_: e.scalar_tensor_tensor(f(UU), f(UU), cst_, f(A2), ALU.add, ALU.mult),
                          (k("uu"), k("a2")), (k("uu"),)))
            if full:
                S.append(("dve", lambda e: e.scalar_tensor_tensor(f(T1), f(T1), 1.0, f(GS), ALU.add, ALU.mult),
                          (k("t1"), k("gs")), (k("t1"),)))
            st.append(S)
            S = [("act", lambda e: e.activation(f(UU), f(UU), AF.Ln, scale=-1.0 / 24.0), (k("uu"),), (k("uu"),)),
                 ("act", lambda e: e.activation(f(A2), f(UU), AF.Exp, scale=0.5), (k("uu"),), (k("a2"),))]
            if full:
                S.append(("act", lambda e: e.activation(f(SG), f(T1), AF.Exp, scale=-K2), (k("t1"),), (k("sg"),)))
                S.append(("act", lambda e: e.activation(f(SG), f(SG), AF.Ln, bias=ones[:, 0:1]), (k("sg"), "ones"), (k("sg"),)))
                S.append(("act", lambda e: e.activation(f(SG), f(SG), AF.Exp, scale=-1.0), (k("sg"),), (k("sg"),)))
            st.append(S)
            S = [("dve", lambda e: e.tensor_tensor(f(UU), f(A2), f(II), ALU.mult), (k("a2"), k("ii")), (k("uu"),)),
                 ("dve", lambda e: e.tensor_tensor(f(UU), f(UU), f(XC), ALU.mult), (k("uu"), k("xc")), (k("uu"),)),
                 ("dve", lambda e: e.tensor_tensor(f(T1 if not full else HH, ns), f3(AA)[:, :, 0], hst[:, c, sidx:sidx + ns], ALU.mult),
                  (k("aa"), "hst"), (k("hh"),)),
                 ("dve", lambda e: e.tensor_tensor(f3(UU)[:, :, 0], f3(UU)[:, :, 0], f(T1 if not full else HH, ns), ALU.add),
                  (k("uu"), k("hh")), (k("uu"),)),
                 ("dve", lambda e: e.memset(f3(AA)[:, :, 0], 0.0), (k("hh"),), (k("aa"),)),
                 ("dve", lambda e: e.tensor_tensor_scan(f(HH), f(AA), f(UU), 0.0, ALU.mult, ALU.add), (k("aa"), k("uu")), (k("hh"),)),
                 ("dve", lambda e: e.tensor_copy(hst[:, c, sidx:sidx + ns], f3(HH)[:, :, Ls - 1]), (k("hh"),), ("hst",))]
            if full:
                S.append(("dve", lambda e: e.tensor_tensor(f(GS), f(GS), f(SG), ALU.mult), (k("gs"), k("sg")), (k("gs"),)))
                S.append(("dve", lambda e: e.tensor_tensor(ya[:, c, so:so + tws], f(HH), f(GS), ALU.mult), (k("hh"), k("gs")), (("ya", c),)))
            st.append(S)
            return st

        jobs = [(c, sub) for c in range(8) for sub in range(2)]
        built = {}
        nst = None
        for step in range(len(jobs) * 5 + 16):
            for i, (c, sub) in enumerate(jobs):
                kst = step - (i // 2) * 10 - 3 * (i % 2)
                if kst < 0 or kst > 9:
                    continue
                if i not in built:
                    if sub == 0 and c % 2 == 0:
                        lru_slabs[("xa", c // 2)] = load_slab(wcols(w_in, c * 128, 256), 16, 256)
                        if full:
                            lru_slabs[("ga", c // 2)] = load_slab(wcols(w_in, 1024 + c * 128, 256), 16, 256)
                    built[i] = job(c, sub)
                for (eng, fn, rds, wrs) in built[i][kst]:
                    P.op(eng, fn, reads=rds, writes=wrs)

        P.barrier()
        P.cp(5)
        VT = 0
        o = nblk * 512
        QR, QD, KR = o, o + 256, o + 512
        KT = o + 768
        QB = o + 1024
        TA, TB = o + 1280, o + 1792
        OT = o + 2304
        SM = o + 2816
        ME, RS = o + 2944, o + 3456
        vt = rb(VT, nblk * 1024).rearrange("p (b f) -> p b f", f=1024)
        qr, qd, kr, qb = rb(QR, tw), rb(QD, tw), rb(KR, tw), rb(QB, tw)
        kt = rb(KT, nblk * 128).rearrange("p (b f) -> p b f", f=128)
        smb = rb(SM, 256).rearrange("p (b f) -> p b f", f=128)
        ta, tb, ot, me, rs = rf(TA, tw), rf(TB, tw), rf(OT, tw), rf(ME, tw), rf(RS, tw)
        SALLB = o + 3968
        sallb = rb(SALLB, (tw // 64) * 128).rearrange("p (c f) -> p c f", f=128)
        SM2 = o + 4480
        smb2 = rb(SM2, 256).rearrange("p (b f) -> p b f", f=128)
        smb4 = [smb[:, 0, :], smb[:, 1, :], smb2[:, 0, :], smb2[:, 1, :]]
        assert SM2 + 128 <= RSZ
        cosv, sinv = cs_t[:, 0, 0:tw], cs_t[:, 1, 0:tw]
        for sl in range(4):
            sv, kv = load_slab(wcols(w_in, 4096 + sl * 256, 256), 16, 256)
            for b in range(nblk):
                pv, kpv = psum()
                for kc in range(16):
                    P.op("pe", lambda e, pv=pv, kc=kc, b=b, sv=sv: e.matmul(
                        pv[:, 0:256], xs(kc)[:, b * 128:(b + 1) * 128], sv[:, kc, :], start=(kc == 0), stop=(kc == 15)),
                        reads=(kv, xkey), writes=(kpv,))
                eng = "dve"
                if eng == "act":
                    P.op("act", lambda e, pv=pv, b=b, sl=sl: e.activation(vt[:, b, sl * 256:(sl + 1) * 256], pv[:, 0:256], AF.Identity),
                         reads=(kpv,), writes=(("vt", b, sl),))
                else:
                    P.op("dve", lambda e, pv=pv, b=b, sl=sl: e.tensor_copy(vt[:, b, sl * 256:(sl + 1) * 256], pv[:, 0:256]),
                         reads=(kpv,), writes=(("vt", b, sl),))

        P.cp(5.1)

        def rope(slab, kslab, col0, dst, kdst):
            pq, kpq = psum()
            proj(slab, kslab, col0, pq, kpq)
            P.op("dve", lambda e: e.tensor_copy(qb, pq[:, 0:tw]), reads=(kpq,), writes=("qb",))
            pp, kpp = psum()
            P.op("pe", lambda e: e.matmul(pp[:, 0:tw], perm[:], qb, start=True, stop=True),
                 reads=("perm", "qb"), writes=(kpp,))
            P.op("dve", lambda e: e.tensor_tensor(ta, pq[:, 0:tw], cosv, ALU.mult), reads=(kpq, "cs"), writes=("ta",))
            P.op("dve", lambda e: e.tensor_tensor(tb, pp[:, 0:tw], sinv, ALU.mult), reads=(kpp, "cs"), writes=("tb",))
            P.op("dve", lambda e: e.tensor_tensor(dst, ta, tb, ALU.add), reads=("ta", "tb"), writes=(kdst,))

        nch = L // 64
        prevC = [None]
        for hp in range(4):
            if full:
                sq_, ksq = load_slab(wcols(w_in, 2048 + hp * 256, 256), 16, 256)
            sk_, ksk = load_slab(wcols(w_in, 3072 + hp * 256, 256), 16, 256)
            if full:
                sg_, ksg = load_slab(wcols(w_in, 5120 + hp * 256, 256), 16, 256)
            for hh in range(2):
                hd = hp * 2 + hh
                col0 = hh * 128
                P.cap = []
                if full:
                    rope(sq_, ksq, col0, qr, "qr")
                    P.op("dve", lambda e, hd=hd: e.tensor_tensor(
                        qd.rearrange("p (c i) -> p c i", i=64), qr.rearrange("p (c i) -> p c i", i=64),
                        qdec[:, hd:hd + 1, :].broadcast_to([128, tw // 64, 64]), ALU.mult),
                        reads=("qr", "qdec"), writes=("qd",))
                rope(sk_, ksk, col0, kr, "kr")
                P.cp(5.2)
                for b in range(nblk):
                    pk, kpk = psum()
                    P.op("pe", lambda e, pk=pk, b=b: e.matmul(pk[:, 0:128], kr[:, b * 128:(b + 1) * 128], ident[:],
                                                              start=True, stop=True),
                         reads=("kr", "ident"), writes=(kpk,))
                    P.op("dve", lambda e, pk=pk, b=b, hd=hd: e.tensor_scalar(kt[:, b, :], pk[:, 0:128], kdecP[:, hd:hd + 1], None, ALU.mult),
                         reads=(kpk, "kdecP"), writes=(("kt", b),))
                P.cp(5.3)
                nchT = tw // 64
                pkv = []
                kvb = [psum(), psum()]
                for ch in range(nchT):
                    bank, kbank = kvb[ch % 2]
                    pkv.append((bank[:, (ch // 2) * 128:(ch // 2 + 1) * 128], kbank))
                    b = ch // 2
                    lo = (ch % 2) * 64
                    P.op("pe", lambda e, dstp=pkv[ch][0], b=b, lo=lo, hd=hd: e.matmul(
                        dstp, kt[lo:lo + 64, b, :], vt[lo:lo + 64, b, hd * 128:(hd + 1) * 128], start=True, stop=True,
                        skip_group_check=True),
                        reads=(("kt", b),) + tuple(("vt", b, s_) for s_ in range(4)), writes=(kbank,))
                for ch in range(nchT):
                    j = (ch * 64) // L
                    swt, ksw0 = sw_of_seq(j)
                    ksw = (ksw0, hd)
                    if full and (ch * 64) % L == 0:
                        P.op("dve", lambda e, swt=swt, hd=hd, ch=ch: e.tensor_copy(sallb[:, ch, :], swt[:, hd, :]),
                             reads=(ksw, ksw0), writes=(("sallb", ch),))
                    P.op("dve", lambda e, srcp=pkv[ch][0], swt=swt, hd=hd: e.scalar_tensor_tensor(
                        swt[:, hd, :], swt[:, hd, :], G64[hd], srcp, ALU.mult, ALU.add),
                        reads=(pkv[ch][1], ksw, ksw0), writes=(ksw,))
                    if full and ch + 1 < nchT and ((ch + 1) * 64) // L == j:
                        P.op("dve", lambda e, swt=swt, hd=hd, ch=ch: e.tensor_copy(sallb[:, ch + 1, :], swt[:, hd, :]),
                             reads=(ksw,), writes=(("sallb", ch + 1),))
                partA, P.cap = P.cap, None
                partC = prevC[0] or []
                prevC[0] = None
                for i_ in range(max(len(partA), len(partC))):
                    if i_ < len(partC):
                        P.op(*partC[i_][:2], reads=partC[i_][2], writes=partC[i_][3])
                    if i_ < len(partA):
                        P.op(*partA[i_][:2], reads=partA[i_][2], writes=partA[i_][3])
                if full:
                    psl = []
                    for b in range(nblk):
                        ps_, kps = psum()
                        psl.append((ps_, kps))
                        P.op("pe", lambda e, ps_=ps_, b=b: e.matmul(ps_[:, 0:128], kr[:, b * 128:(b + 1) * 128],
                                                                    qr[:, b * 128:(b + 1) * 128], start=True, stop=True),
                             reads=("kr", "qr"), writes=(kps,))
                    for b in range(nblk):
                        ps_, kps = psl[b]
                        P.op("dve", lambda e, ps_=ps_, b=b, hd=hd: e.tensor_tensor(smb4[b], ps_[:, 0:128], dmT[:, hd, :], ALU.mult),
                             reads=(kps, "dmT"), writes=(("smb", b),))
                    pol = []
                    for b in range(nblk):
                        po, kpo = psum()
                        pol.append((po, kpo))
                        P.op("pe", lambda e, po=po, b=b, hd=hd: e.matmul(
                            po[:, 0:128], vt[:, b, hd * 128:(hd + 1) * 128], smb4[b], start=True, stop=False, skip_group_check=True),
                            reads=(("smb", b),) + tuple(("vt", b, s_) for s_ in range(4)), writes=(kpo,))
                        for ch2 in range(2):
                            ch = 2 * b + ch2
                            P.op("pe", lambda e, po=po, ch=ch, ch2=ch2: e.matmul(
                                po[:, ch2 * 64:(ch2 + 1) * 64], sallb[:, ch, :], qd[:, ch * 64:(ch + 1) * 64], start=False, stop=(ch2 == 1),
                                skip_group_check=True),
                                reads=(("sallb", ch), "qd"), writes=(kpo,))
                    for b in range(nblk):
                        po, kpo = pol[b]
                        P.op("act", lambda e, po=po, b=b: e.activation(ot[:, b * 128:(b + 1) * 128], po[:, 0:128], AF.Identity),
                             reads=(kpo,), writes=("ot",))
                if full:
                    P.cap = []
                    pm, kpm = psum()
                    pq2, kpq2 = psum()
                    P.op("pe", lambda e, pm=pm: e.matmul(pm[:, 0:tw], ones[:], ot, start=True, stop=True),
                         reads=("ones", "ot"), writes=(kpm,))
                    P.op("act", lambda e: e.activation(rs, ot, AF.Square), reads=("ot",), writes=("rs",))
                    P.op("pe", lambda e, pq2=pq2: e.matmul(pq2[:, 0:tw], ones[:], rs, start=True, stop=True),
                         reads=("ones", "rs"), writes=(kpq2,))
                    P.op("dve", lambda e, pm=pm: e.tensor_scalar(me, pm[:, 0:tw], 1.0 / 128, None, ALU.mult), reads=(kpm,), writes=("me",))
                    P.op("dve", lambda e: e.tensor_tensor(rs, me, me, ALU.mult), reads=("me", kpq2), writes=("rs",))
                    P.op("dve", lambda e, pq2=pq2: e.scalar_tensor_tensor(rs, pq2[:, 0:tw], 1.0 / 128, rs, ALU.mult, ALU.subtract),
                         reads=(kpq2, "rs"), writes=("rs",))
                    P.op("dve", lambda e: e.tensor_scalar(rs, rs, 0.0, 1e-5, ALU.max, ALU.add), reads=("rs",), writes=("rs",))
                    P.op("act", lambda e: e.activation(rs, rs, AF.Ln), reads=("rs",), writes=("rs",))
                    P.op("act", lambda e: e.activation(rs, rs, AF.Exp, scale=-0.5), reads=("rs",), writes=("rs",))
                    P.op("dve", lambda e: e.tensor_tensor(ot, ot, me, ALU.subtract), reads=("ot", "me"), writes=("ot",))
                    P.op("dve", lambda e: e.tensor_tensor(ot, ot, rs, ALU.mult), reads=("ot", "rs"), writes=("ot",))
                    P.op("act", lambda e, hd=hd: e.activation(ot, ot, AF.Identity, bias=vec[:, V_GNB + hd:V_GNB + hd + 1],
                                                               scale=vec[:, V_GNG + hd:V_GNG + hd + 1]),
                         reads=("ot", "vec"), writes=("ot",))
                    pg, kpg = psum()
                    proj(sg_, ksg, col0, pg, kpg)
                    P.op("act", lambda e, pg=pg: e.activation(me, pg[:, 0:tw], AF.Exp, scale=-1.0), reads=(kpg, "ot"), writes=("me",))
                    P.op("act", lambda e: e.activation(me, me, AF.Ln, bias=ones[:, 0:1]), reads=("me", "ones"), writes=("me",))
                    P.op("act", lambda e: e.activation(me, me, AF.Exp, scale=-1.0), reads=("me",), writes=("me",))
                    P.op("dve", lambda e, pg=pg: e.tensor_tensor(me, me, pg[:, 0:tw], ALU.mult), reads=("me", kpg), writes=("me",))
                    P.op("dve", lambda e, hd=hd: e.tensor_tensor(yb[:, hd, 0:tw], ot, me, ALU.mult),
                         reads=("me", "ot"), writes=(("yb", hd),))
                    prevC[0], P.cap = P.cap, None
        for it_ in (prevC[0] or []):
            P.op(*it_[:2], reads=it_[2], writes=it_[3])
        prevC[0] = None
        P.barrier()
        P.cp(6)
        if not full:
            return
        merged = rb(0, 16 * tw).rearrange("p (c t) -> p c t", t=tw)
        SA, SB_, M1 = 4096, 4608, 5120
        sa, sb_, m1 = rf(SA, tw), rf(SB_, tw), rf(M1, tw)
        for sp in range(8):
            sga_, kga_ = load_slab(wcols(w_in, 6144 + sp * 256, 256), 16, 256)
            sgb_, kgb_ = load_slab(wcols(w_in, 8192 + sp * 256, 256), 16, 256)
            swa, kwa = load_slab(wcols(W["w_a"], sp * 256, 256), 8, 256)
            swb, kwb = load_slab(wcols(W["w_b"], sp * 256, 256), 8, 256)
            for cc in range(2):
                dc = sp * 2 + cc
                col0 = cc * 128
                pa, kpa = psum()
                proj(sga_, kga_, col0, pa, kpa)
                pb, kpb = psum()
                proj(sgb_, kgb_, col0, pb, kpb)
                pya, kpya = psum()
                for c in range(8):
                    P.op("pe", lambda e, c=c, pya=pya, swa=swa, col0=col0: e.matmul(pya[:, 0:tw], swa[:, c, col0:col0 + 128], ya[:, c, 0:tw],
                                                                start=(c == 0), stop=(c == 7)),
                         reads=(kwa, ("ya", c)), writes=(kpya,))
                pyb, kpyb = psum()
                for c in range(8):
                    P.op("pe", lambda e, c=c, pyb=pyb, swb=swb, col0=col0: e.matmul(pyb[:, 0:tw], swb[:, c, col0:col0 + 128], yb[:, c, 0:tw],
                                                                start=(c == 0), stop=(c == 7)),
                         reads=(kwb, ("yb", c)), writes=(kpyb,))
                P.op("act", lambda e, pa=pa: e.activation(sa, pa[:, 0:tw], AF.Sigmoid), reads=(kpa,), writes=("sa",))
                P.op("act", lambda e, pb=pb: e.activation(sb_, pb[:, 0:tw], AF.Sigmoid), reads=(kpb,), writes=("sb",))
                P.op("dve", lambda e, pya=pya: e.tensor_tensor(m1, sa, pya[:, 0:tw], ALU.mult), reads=("sa", kpya), writes=("m1",))
                P.op("dve", lambda e, pyb=pyb: e.tensor_tensor(sb_, sb_, pyb[:, 0:tw], ALU.mult), reads=("sb", kpyb), writes=("sb",))
                P.op("dve", lambda e, dc=dc: e.tensor_tensor(merged[:, dc, :], m1, sb_, ALU.add), reads=("m1", "sb"),
                     writes=(("mg", dc),))
        for sp in range(8):
            swo, kwo = load_slab(wcols(W["w_o"], sp * 256, 256), 16, 256)
            for cc in range(2):
                dc2 = sp * 2 + cc
                pw, kpw = psum()
                for dc in range(16):
                    P.op("pe", lambda e, dc=dc, pw=pw, cc=cc, swo=swo: e.matmul(pw[:, 0:tw], swo[:, dc, cc * 128:(cc + 1) * 128], merged[:, dc, :],
                                                                        start=(dc == 0), stop=(dc == 15)),
                         reads=(kwo, ("mg", dc)), writes=(kpw,))
                P.op("dve", lambda e, pw=pw, dc2=dc2: e.tensor_tensor(acc[:, dc2, t0g:t0g + tw], acc[:, dc2, t0g:t0g + tw], pw[:, 0:tw], ALU.add),
                     reads=(kpw, ("acc", dc2)), writes=(("acc", dc2),))
        if tile_id == 0 and tw == 512:
            for c in range(8):
                dump(c, ya[:, c, 0:512], [("ya", c)])
                dump(8 + c, yb[:, c, 0:512], [("yb", c)])
            for dc in range(16):
                dump(16 + dc, merged[:, dc, :], [("mg", dc)])
                dump(32 + dc, acc[:, dc, t0g:t0g + 512], [("acc", dc)])
                dump(48 + dc, xs(dc), [xkey])
        P.barrier()

    try:
        P.cp(1)
        def ffn_phase(xT, T, tiles, wg, wu, wd, li, xb=None, skip_load=False, final=False, x_prefetched=False):
            with ExitStack() as ph:
                def sbp(name, shape, dt=F32):
                    return ph.enter_context(sbt(name, list(shape), dt))
                if xb is None:
                    xb = sbp("xb", [128, NDC, TM], BF16)
                with ExitStack() as ph2:
                    slabs["t"] = [ph2.enter_context(sbt("slab%d" % i, [128, 4096], BF16)) for i in range(6)]
                    slabs["i"] = 0
                    hg = [ph2.enter_context(sbt("hg%d" % i, [128, 2, TM], BF16)) for i in range(2)]
                    sgt = [ph2.enter_context(sbt("sgt%d" % i, [128, 512], F32)) for i in range(2)]
                    if not skip_load:
                        load_x(xT, T, xb, dma=not x_prefetched)
                    else:
                        for dc in range(NDC):
                            P.op("act", lambda e, dc=dc: e.activation(xb[:, dc, 0:T], acc[:, dc, 0:T], AF.Identity, scale=1.0 / ALPHA),
                                 reads=(("acc", dc),), writes=(("xb", dc),))
                    P.cp(2)
                    ffn(tiles, wg, wu, wd, xb, hg, sgt)
                    P.cp(3)
                    P.barrier(keep_prefix=("@none@",), skip=())
                with ExitStack() as ph3:
                    def sb3(name, shape):
                        return ph3.enter_context(sbt(name, list(shape), F32))
                    lnt = ([ph3.enter_context(sbt("lnsq%d" % i, [128, 512], BF16)) for i in range(2)], [sb3("lnmean%d" % i, [128, 512]) for i in range(3)],
                           [sb3("lnrstd%d" % i, [128, 512]) for i in range(3)], [sb3("lnt%d" % i, [128, 512]) for i in range(2)])
                    layernorm(tiles, li, xb, lnt, final=final)
                    P.barrier(keep_prefix=("@none@",), skip=())
                    P.cp(4)

        with ExitStack() as php:
            xbp = php.enter_context(sbt("xbp", [128, NDC, TP], BF16))
            ffn_phase(xpT, TP, TILES_P, W["f1g"], W["f1u"], W["f1d"], 0, xb=xbp)
            load_x(xmT, TM, None, dma=True, conv=False)
            slabs["t"] = [php.enter_context(sbt("pslab%d" % i, [128, 4096], BF16)) for i in range(4)]
            slabs["i"] = 0
            Rp = php.enter_context(sbt("Rp", [128, RSZ], F32))
            cs_p = php.enter_context(sbt("cstp", [128, 2, 512], F32))
            for ti, (t0, tw) in enumerate(TILES_P):
                mixer_tile(lambda kc, t0=t0, tw=tw: xbp[:, kc, t0:t0 + tw], "xball", t0, TM + t0, tw, 1, 512, 0,
                           False, Rp, cs_p, None, None, lambda j: (Sw[0], ("Sw", 0)), ti)
            P.barrier(keep_prefix=("@none@",), skip=())

        P.op("dve", lambda e: e.tensor_scalar(Sw[0][:].rearrange("p h f -> p (h f)"), Sw[0][:].rearrange("p h f -> p (h f)"),
                                              mask[:, 0:1], None, ALU.mult), reads=(("Sw", 0), "mask") + tuple((("Sw", 0), h_) for h_ in range(H)), writes=(("Sw", 0),) + tuple((("Sw", 0), h_) for h_ in range(H)))
        P.op("dve", lambda e: e.tensor_scalar(hst[:].rearrange("p c s -> p (c s)"), hst[:].rearrange("p c s -> p (c s)"),
                                              mask[:, 0:1], None, ALU.mult), reads=("hst", "mask"), writes=("hst",))
        P.op("dve", lambda e: e.tensor_scalar(cst[:].rearrange("p c s j -> p (c s j)"), cst[:].rearrange("p c s j -> p (c s j)"),
                                              mask[:, 0:1], None, ALU.mult), reads=("cst", "mask"), writes=("cst",))
        P.dma("sp", [lambda e: e.dma_start(out=hst[:, :, 1:5], in_=stlT.rearrange("p (c s) -> p c s", s=4))], "st0",
              reads=(), writes=("hst",))
        P.dma("sp", [lambda e: e.dma_start(out=cst[:, :, 1:5, :], in_=stcT.rearrange("p (c s j) -> p c s j", s=4, j=3))], "st1",
              reads=(), writes=("cst",))

        ffn_phase(xmT, TM, TILES_M, W["f1g"], W["f1u"], W["f1d"], 0, x_prefetched=True)

        with ExitStack() as ph:
            def sbp(name, shape, dt=F32):
                return ph.enter_context(sbt(name, list(shape), dt))
            slabs["t"] = [sbp("mslab%d" % i, [128, 4096], BF16) for i in range(4)]
            slabs["i"] = 0
            R = sbp("R", [128, RSZ])
            cs_t = sbp("cst_", [128, 2, 512])
            xbt = sbp("xbt", [128, NDC, 512], BF16)
            ya = sbp("ya", [128, 8, 512], BF16)
            yb = sbp("yb", [128, 8, 512], BF16)
            for ti, (t0, tw) in enumerate(TILES_M):
                for dc in range(NDC):
                    eng = "act" if dc % 2 == 0 else "dve"
                    if eng == "act":
                        P.op("act", lambda e, dc=dc, t0=t0, tw=tw: e.activation(xbt[:, dc, 0:tw], acc[:, dc, t0:t0 + tw], AF.Identity, scale=1.0 / ALPHA),
                             reads=(("acc", dc),), writes=(("xball", dc),))
                    else:
                        P.op("dve", lambda e, dc=dc, t0=t0, tw=tw: e.tensor_scalar(xbt[:, dc, 0:tw], acc[:, dc, t0:t0 + tw], 1.0 / ALPHA, None, ALU.mult),
                             reads=(("acc", dc),), writes=(("xball", dc),))
                P.barrier()
                if ti < 2:
                    sw_of = lambda j: (Sw[0], ("Sw", 0))
                    mixer_tile(lambda kc, tw=tw: xbt[:, kc, 0:tw], "xball", t0, t0, tw, 1, 512, 0, True, R, cs_t, ya, yb, sw_of, ti)
                    if ti == 1:
                        P.dma("sp", [lambda e: e.dma_start(out=S_o[0].rearrange("h d f -> d h f"), in_=Sw[0][:])], "so0",
                              reads=(("Sw", 0),))
                else:
                    Sx = [Sw[0], sbp("Sw1", [128, H, 128]), sbp("Sw2", [128, H, 128]), sbp("Sw3", [128, H, 128])]
                    for j in range(4):
                        P.dma("sp", [lambda e, j=j: e.dma_start(out=Sx[j][:], in_=strt[j].rearrange("h d f -> d h f"))], "sl%d" % j,
                              reads=(), writes=(("Sw", j),))
                    sw_of = lambda j: (Sx[j], ("Sw", j))
                    mixer_tile(lambda kc, tw=tw: xbt[:, kc, 0:tw], "xball", t0, t0, tw, 4, 64, 1, True, R, cs_t, ya, yb, sw_of, ti)
                    for j in range(4):
                        P.dma("sp", [lambda e, j=j: e.dma_start(out=S_o[1 + j].rearrange("h d f -> d h f"), in_=Sx[j][:])], "so%d" % (1 + j),
                              reads=(("Sw", j),))
            P.dma("sp", [lambda e: e.dma_start(out=hoT[:, :], in_=hst[:].rearrange("p c s -> p (c s)"))], "ho", reads=("hst",))
            P.dma("sp", [lambda e: e.dma_start(out=coT[:, :], in_=cst[:].rearrange("p c s j -> p (c s j)"))], "co", reads=("cst",))
            P.barrier(keep_prefix=("@none@",), skip=())

        with ExitStack() as ph:
            xb2 = ph.enter_context(sbt("xb2", [128, NDC, TM], BF16))
            sq2 = [ph.enter_context(sbt("sq2_%d" % i, [128, 512], BF16)) for i in range(2)]
            lnt = (sq2, [ph.enter_context(sbt("lnm2_%d" % i, [128, 512], F32)) for i in range(3)],
                   [ph.enter_context(sbt("lnr2_%d" % i, [128, 512], F32)) for i in range(3)],
                   [ph.enter_context(sbt("lnt2_%d" % i, [128, 512], F32)) for i in range(2)])
            layernorm(TILES_M, 1, xb2, lnt)
            P.barrier(keep_prefix=("@none@",), skip=())
        ffn_phase(None, TM, TILES_M, W["f2g"], W["f2u"], W["f2d"], 2, skip_load=True, final=True)


    except _Stop:
        pass

    semnames = set()
    for e in ENGS:
        for waits, fn, inc in P.ops[e]:
            for s, _ in waits:
                semnames.add(s)
            if inc is not None:
                semnames.add(inc[0])
    sems = {}
    for s in sorted(semnames):
        sems[s] = es.enter_context(nc.semaphore(s))
    fin = [("E" + e, P.cnt[e]) for e in ENGS if P.cnt[e] > 0 and e != "sp"] + [("D" + s, v) for s, v in P.dsem.items()]
    P.ops["sp"].append((fin, None, None))

    engmap = {"pe": "tensor", "act": "scalar", "dve": "vector", "pool": "gpsimd", "sp": "sync"}
    with nc.Block() as block:
        def make(ename):
            def body(eng):
                for waits, fn, inc in P.ops[ename]:
                    for s, v in waits:
                        eng.wait_ge(sems[s], v)
                    if fn is not None:
                        ins = fn(eng)
                        ins.then_inc(sems[inc[0]], inc[1])
            return body
        for ename in ENGS:
            getattr(block, engmap[ename])(make(ename))
    try:
        es.close()
    except AssertionError:
        pass
    return nc


_NC_CACHE = {}


def _rows_to_T(v, ncol):
    return np.ascontiguousarray(v.reshape(ncol, 128).T)


def kernel(x_prompt, x_sample, state_conv, state_lru, state_ret,
           ffn1_w_gate, ffn1_w_up, ffn1_w_down, ln1_g, ln1_b,
           w_in, conv_w, conv_b, rg_w, rg_b, ig_w, ig_b, lru_lambda,
           ret_gn_g, ret_gn_b, w_a_proj, w_b_proj, w_o, ln2_g, ln2_b,
           ffn2_w_gate, ffn2_w_up, ffn2_w_down, ln3_g, ln3_b):
    f = lambda a: np.ascontiguousarray(np.asarray(a, dtype=np.float32))
    x_prompt, x_sample = f(x_prompt), f(x_sample)
    state_conv, state_lru, state_ret = f(state_conv), f(state_lru), f(state_ret)
    if "nc" not in _NC_CACHE:
        _NC_CACHE["nc"] = build_nc()
    nc = _NC_CACHE["nc"]

    vecT = np.zeros((128, NV), np.float32)
    for i, v in enumerate([ln1_g, ln1_b, ln2_g, ln2_b, ln3_g, ln3_b]):
        vecT[:, i * 16:(i + 1) * 16] = _rows_to_T(f(v)[0], 16)
    cw = f(conv_w)[0]
    for c in range(8):
        for j in range(4):
            vecT[:, V_CW + c * 4 + j] = cw[j, c * 128:(c + 1) * 128]
    vecT[:, V_CB:V_CB + 8] = _rows_to_T(f(conv_b)[0], 8)
    vecT[:, V_RGB:V_RGB + 8] = _rows_to_T(f(rg_b)[0], 8)
    vecT[:, V_IGB:V_IGB + 8] = _rows_to_T(f(ig_b)[0], 8)
    vecT[:, V_LAM:V_LAM + 8] = _rows_to_T(f(lru_lambda)[0], 8)
    vecT[:, V_GNG:V_GNG + 8] = _rows_to_T(f(ret_gn_g)[0], 8)
    vecT[:, V_GNB:V_GNB + 8] = _rows_to_T(f(ret_gn_b)[0], 8)
    bdm = np.zeros((128, 16, 128), np.float32)
    for c in range(8):
        for gi, w in enumerate([f(rg_w)[0], f(ig_w)[0]]):
            bdm[0:64, 2 * c + gi, 0:64] = w[2 * c]
            bdm[64:128, 2 * c + gi, 64:128] = w[2 * c + 1]
    bdm = bdm.reshape(128, 16 * 128)
    ident = np.eye(128, dtype=np.float32)
    perm = np.zeros((128, 128), np.float32)
    for m in range(128):
        perm[(m + 64) % 128, m] = 1.0
    sc = np.float32(128.0 ** -0.5)
    idx = np.arange(64, dtype=np.float32)
    dmT = np.zeros((128, H, 128), np.float32)
    qdec = np.zeros((128, H, 64), np.float32)
    kdecP = np.zeros((128, H), np.float32)
    for h in range(H):
        lg = np.float32(np.log1p(-np.exp2(np.float32(-5.0 - h))))
        diff = idx[None, :] - idx[:, None]
        m = np.where(diff >= 0, np.exp(lg * np.maximum(diff, 0.0)), 0.0).astype(np.float32) * sc
        dmT[0:64, h, 0:64] = m
        dmT[64:128, h, 64:128] = m
        qdec[:, h, :] = np.exp(lg * (idx + 1.0))[None, :]
        kd = np.exp(lg * (63.0 - idx)).astype(np.float32) * sc
        kdecP[0:64, h] = kd
        kdecP[64:128, h] = kd
    inv_freq = (np.float32(10000.0) ** (-np.arange(0, 128, 2, dtype=np.float32) / np.float32(128))).astype(np.float32)
    invp = np.concatenate([inv_freq, inv_freq])
    sgn = np.concatenate([-np.ones(64, np.float32), np.ones(64, np.float32)])
    shared = {
        "vecT": vecT, "bd": bdm, "ident": ident, "perm": perm,
        "dmT": dmT.reshape(128, H * 128), "qdec": qdec.reshape(128, H * 64), "kdecP": kdecP,
        "f1g": f(ffn1_w_gate)[0], "f1u": f(ffn1_w_up)[0], "f1d": f(ffn1_w_down)[0], "w_in": f(w_in)[0],
        "w_a": f(w_a_proj)[0], "w_b": f(w_b_proj)[0], "w_o": f(w_o)[0],
        "f2g": f(ffn2_w_gate)[0], "f2u": f(ffn2_w_up)[0], "f2d": f(ffn2_w_down)[0],
    }
    in_maps = []
    for c in range(8):
        k, half = c // 2, c % 2
        xm = np.concatenate([x_prompt[k, half * 1024:(half + 1) * 1024], x_sample[4 * c:4 * c + 4].reshape(256, D)], 0)
        xp = x_prompt[k, 0:1024] if half else np.zeros((1024, D), np.float32)
        pos = np.concatenate([half * 1024 + np.arange(1024), np.tile(2048 + np.arange(64), 4), np.arange(1024)]).astype(np.float32)
        ang = (pos[None, :] * invp[:, None]).astype(np.float32)
        cs = np.stack([np.cos(ang), np.sin(ang) * sgn[:, None]], axis=1).astype(np.float32)
        stc = state_conv[0, 4 * c:4 * c + 4]
        stcT = stc.reshape(4, 3, 8, 128).transpose(3, 2, 0, 1).reshape(128, 96)
        stl = state_lru[0, 4 * c:4 * c + 4]
        stlT = stl.reshape(4, 8, 128).transpose(2, 1, 0).reshape(128, 32)
        m = dict(shared)
        m.update({
            "xmT": np.ascontiguousarray(xm.T), "xpT": np.ascontiguousarray(xp.T),
            "mask": np.full((128, 1), float(half), np.float32), "cs": np.ascontiguousarray(cs),
            "stcT": np.ascontiguousarray(stcT), "stlT": np.ascontiguousarray(stlT),
            "strt": np.ascontiguousarray(state_ret[0, 4 * c:4 * c + 4]),
        })
        in_maps.append(m)
    res = run_bass_kernel_spmd(nc, in_maps, core_ids=list(range(8)))
    R = res.results
    if os.environ.get("KDEBUG"):
        globals()["_DBG"] = [r["dbg"] for r in R]

    y_prompt = np.zeros((4, 2048, D), np.float32)
    y_sample = np.zeros((32, 64, D), np.float32)
    conv_p = np.zeros((1, 4, 3, 1024), np.float32)
    lru_p = np.zeros((1, 4, 1024), np.float32)
    ret_p = np.zeros((1, 4, H, 128, 128), np.float32)
    conv_s = np.zeros((1, 32, 3, 1024), np.float32)
    lru_s = np.zeros((1, 32, 1024), np.float32)
    ret_s = np.zeros((1, 32, H, 128, 128), np.float32)
    for c in range(8):
        k, half = c // 2, c % 2
        yT = R[c]["yT"]
        y_prompt[k, half * 1024:(half + 1) * 1024] = yT[:, 0:1024].T
        y_sample[4 * c:4 * c + 4] = yT[:, 1024:1280].T.reshape(4, 64, D)
        ho = R[c]["hoT"].reshape(128, 8, 5)
        co = R[c]["coT"].reshape(128, 8, 5, 3)
        So = R[c]["S_o"]
        hrow = ho.transpose(2, 1, 0).reshape(5, 1024)
        crow = co.transpose(2, 3, 1, 0).reshape(5, 3, 1024)
        if half == 1:
            conv_p[0, k] = crow[0]
            lru_p[0, k] = hrow[0]
            ret_p[0, k] = So[0]
        conv_s[0, 4 * c:4 * c + 4] = crow[1:5]
        lru_s[0, 4 * c:4 * c + 4] = hrow[1:5]
        ret_s[0, 4 * c:4 * c + 4] = So[1:5]
    return (y_prompt, y_sample, conv_p, lru_p, ret_p, conv_s, lru_s, ret_s)
```

```python
import math
from contextlib import ExitStack

import numpy as np
import concourse.bass as bass
import concourse.mybir as mybir
from concourse.bass_utils import run_bass_kernel_spmd

F32 = mybir.dt.float32
BF16 = mybir.dt.bfloat16
ALU = mybir.AluOpType
AF = mybir.ActivationFunctionType

D = 2048
DFF = 5632
DIN = 10240
NDC = 16
NG = 22
TM = 1280
TP = 1024
TILES_M = [(0, 512), (512, 512), (1024, 256)]
TILES_P = [(0, 512), (512, 512)]
ALPHA = 2.0 ** 0.25
LN_EPS = 1e-5
H = 8
RSZ = 6656
SELF_WAIT = {"pe": False, "act": True, "dve": True, "pool": True, "sp": False}
ENGS = ["pe", "act", "dve", "pool", "sp"]
LOGG = [math.log1p(-2.0 ** (-5.0 - h)) for h in range(H)]
G64 = [math.exp(LOGG[h] * 64.0) for h in range(H)]
NV = 176
V_LN = 0
V_CW = 96
V_CB = 128
V_RGB = 136
V_IGB = 144
V_LAM = 152
V_GNG = 160
V_GNB = 168


import os
LIMIT = float(os.environ.get("KLIMIT", "999"))


class _Stop(Exception):
    pass


class Prog:
    def cp(self, n):
        if n > LIMIT:
            raise _Stop()

    def __init__(self):
        self.ops = {e: [] for e in ENGS}
        self.cnt = {e: 0 for e in ENGS}
        self.known = {e: {} for e in ENGS}
        self.wr = {}
        self.rd = {}
        self.dsem = {}

    def _deps(self, eng, reads, writes):
        deps = {}

        def add(tok):
            s, v = tok
            if s == "E" + eng and not SELF_WAIT[eng]:
                return
            if deps.get(s, 0) < v:
                deps[s] = v

        for k in reads:
            if k in self.wr:
                add(self.wr[k])
        for k in writes:
            if k in self.wr:
                add(self.wr[k])
            for tok in self.rd.get(k, {}).items():
                add(tok)
        kn = self.known[eng]
        waits = []
        for s, v in deps.items():
            if kn.get(s, 0) < v:
                kn[s] = v
                waits.append((s, v))
        return waits

    def _commit(self, reads, writes, tok):
        s, v = tok
        for k in reads:
            d = self.rd.setdefault(k, {})
            if d.get(s, 0) < v:
                d[s] = v
        for k in writes:
            self.wr[k] = tok
            self.rd[k] = {}

    cap = None

    def op(self, eng, fn, reads=(), writes=()):
        if self.cap is not None:
            self.cap.append((eng, fn, reads, writes))
            return
        waits = self._deps(eng, reads, writes)
        self.cnt[eng] += 1
        tok = ("E" + eng, self.cnt[eng])
        self.ops[eng].append((waits, fn, ("E" + eng, 1)))
        self._commit(reads, writes, tok)

    def dma(self, q, fns, sem, reads=(), writes=()):
        waits = self._deps(q, reads, writes)
        for i, fn in enumerate(fns):
            self.ops[q].append((waits if i == 0 else [], fn, ("D" + sem, 16)))
        self.dsem[sem] = self.dsem.get(sem, 0) + 16 * len(fns)
        self._commit(reads, writes, ("D" + sem, self.dsem[sem]))

    def barrier(self, keep_prefix=("slab",), skip=("pool",)):
        toks = {"E" + e: self.cnt[e] for e in ENGS if self.cnt[e] > 0}
        for s, v in self.dsem.items():
            if not s.startswith(keep_prefix):
                toks["D" + s] = v
        for e in ENGS:
            if e in skip:
                continue
            kn = self.known[e]
            waits = []
            for s, v in toks.items():
                if s == "E" + e and not SELF_WAIT[e]:
                    continue
                if kn.get(s, 0) < v:
                    kn[s] = v
                    waits.append((s, v))
            if waits:
                self.ops[e].append((waits, None, None))
        self.wr = {k: v for k, v in self.wr.items() if isinstance(k, tuple) and str(k[0]).startswith(keep_prefix)}
        self.rd = {k: v for k, v in self.rd.items() if isinstance(k, tuple) and str(k[0]).startswith(keep_prefix)}


def build_nc():
    nc = bass.Bass("TRN2", target_bir_lowering=False)
    P = Prog()

    def din(name, shape):
        return nc.dram_tensor(name, list(shape), F32, kind="ExternalInput").ap()

    def dout(name, shape):
        return nc.dram_tensor(name, list(shape), F32, kind="ExternalOutput").ap()

    xmT = din("xmT", [D, TM])
    xpT = din("xpT", [D, TP])
    maskd = din("mask", [128, 1])
    csd = din("cs", [128, 2, TM + TP])
    stcT = din("stcT", [128, 8 * 4 * 3])
    stlT = din("stlT", [128, 8 * 4])
    strt = din("strt", [4, H, 128, 128])
    vecd = din("vecT", [128, NV])
    bdd = din("bd", [128, 16 * 128])
    identd = din("ident", [128, 128])
    permd = din("perm", [128, 128])
    dmTd = din("dmT", [128, H * 128])
    qdecd = din("qdec", [128, H * 64])
    kdecd = din("kdecP", [128, H])
    W = {}
    for nm, shp in [("f1g", [D, DFF]), ("f1u", [D, DFF]), ("f1d", [DFF, D]), ("w_in", [D, DIN]),
                    ("w_a", [1024, D]), ("w_b", [1024, D]), ("w_o", [D, D]),
                    ("f2g", [D, DFF]), ("f2u", [D, DFF]), ("f2d", [DFF, D])]:
        W[nm] = din(nm, shp)
    yT = dout("yT", [D, TM])
    hoT = dout("hoT", [128, 8 * 5])
    coT = dout("coT", [128, 8 * 5 * 3])
    S_o = dout("S_o", [5, H, 128, 128])
    DEBUG = bool(os.environ.get("KDEBUG"))
    if DEBUG:
        dbg = dout("dbg", [128, 64, 512])

    es = ExitStack()
    uid = [0]

    def sbt(name, shape, dt=F32):
        uid[0] += 1
        return nc.sbuf_tensor("s%d_%s" % (uid[0], name), list(shape), dt)

    def sb(name, shape, dt=F32):
        return es.enter_context(sbt(name, shape, dt))

    acc = sb("acc", [128, NDC, TM])
    vec = sb("vec", [128, NV])
    lnsc = sb("lnsc", [128, 6 * 16])
    bd = sb("bd", [128, 16, 128], BF16)
    ident = sb("ident", [128, 128], BF16)
    perm = sb("perm", [128, 128], BF16)
    ones = sb("ones", [128, 128])
    onesb = sb("onesb", [128, 128], BF16)
    dmT = sb("dmT", [128, H, 128])
    qdec = sb("qdec", [128, H, 64])
    kdecP = sb("kdecP", [128, H])
    c8 = sb("c8", [128, 16])
    ctmp = sb("ctmp", [128, 32])
    negb = sb("negb", [128, 16])
    mask = sb("mask", [128, 1])
    hst = sb("hst", [128, 8, 5])
    cst = sb("cst", [128, 8, 5, 3])
    Sw = [sb("Sw0", [128, H, 128])]
    ps_all = es.enter_context(nc.psum_tensor("psum_all", [128, 8 * 512], F32))

    if DEBUG:
        dstg = sb("dstg", [128, 512])

    def dump(idx, src, keys):
        if not DEBUG:
            return
        P.op("dve", lambda e: e.tensor_copy(dstg[:], src), reads=tuple(keys), writes=("dstg",))
        P.dma("sp", [lambda e: e.dma_start(out=dbg[:, idx, :], in_=dstg[:])], "dbg", reads=("dstg",))

    psi = [0]

    def psum():
        i = psi[0] % 8
        psi[0] += 1
        return ps_all[:, i * 512:(i + 1) * 512], ("ps", i)

    slabs = {"t": [], "i": 0}

    def load_slab(src3, nk, ncol):
        n = len(slabs["t"])
        i = slabs["i"] % n
        slabs["i"] += 1
        t = slabs["t"][i]
        view = t[:, 0:nk * ncol].rearrange("p (c f) -> p c f", f=ncol)
        key = ("slab", i)
        nrow = 128 * nk
        nsplit = max(1, nrow // 1024)
        step = nk // nsplit
        fns = []
        for s in range(nsplit):
            fns.append(lambda e, s=s: e.dma_start(out=view[:, s * step:(s + 1) * step, :],
                                                  in_=src3[:, s * step:(s + 1) * step, :]))
        P.dma("pool", fns, "slab%d" % i, reads=(), writes=(key,))
        return view, key

    def wcols(w, c0, ncol):
        return w.rearrange("(c p) f -> p c f", p=128)[:, :, c0:c0 + ncol]

    def wrows(w, r0, nrow):
        return w[r0:r0 + nrow, :].rearrange("(c p) f -> p c f", p=128)

    def ld(dst, src, sem, key, q="sp"):
        P.dma(q, [lambda e: e.dma_start(out=dst, in_=src)], sem, writes=(key,))

    ld(vec[:], vecd[:, :], "c0", "vec")
    ld(dmT[:].rearrange("p h f -> p (h f)"), dmTd[:, :], "c1", "dmT")
    ld(qdec[:].rearrange("p h f -> p (h f)"), qdecd[:, :], "c2", "qdec")
    ld(kdecP[:], kdecd[:, :], "c3", "kdecP")
    ld(mask[:], maskd[:, :], "c4", "mask")
    ld(bd[:].rearrange("p h f -> p (h f)"), bdd[:, :], "c5", "bd", q="pool")
    ld(ident[:], identd[:, :], "c6", "ident", q="pool")
    ld(perm[:], permd[:, :], "c7", "perm", q="pool")
    P.op("dve", lambda e: e.memset(ones[:], 1.0), writes=("ones",))
    P.op("dve", lambda e: e.tensor_copy(onesb[:], ones[:]), reads=("ones",), writes=("onesb",))
    P.op("dve", lambda e: e.memset(Sw[0][:], 0.0), writes=(("Sw", 0),))
    P.op("dve", lambda e: e.memset(hst[:], 0.0), writes=("hst",))
    P.op("dve", lambda e: e.memset(cst[:], 0.0), writes=("cst",))
    P.op("dve", lambda e: e.tensor_scalar(lnsc[:], vec[:, 0:96], ALPHA, None, ALU.mult),
         reads=("vec",), writes=("lnsc",))
    lam = vec[:, V_LAM:V_LAM + 8]
    yv, y2, y3, sm_ = ctmp[:, 0:8], ctmp[:, 8:16], ctmp[:, 16:24], ctmp[:, 24:32]
    P.op("act", lambda e: e.activation(yv, lam, AF.Exp, scale=-1.0), reads=("vec",), writes=("ct0",))
    P.op("dve", lambda e: e.tensor_tensor(y2, yv, yv, ALU.mult), reads=("ct0",), writes=("ct1",))
    P.op("dve", lambda e: e.tensor_tensor(y3, y2, yv, ALU.mult), reads=("ct0", "ct1"), writes=("ct2",))
    P.op("dve", lambda e: e.scalar_tensor_tensor(sm_, y2, -0.5, yv, ALU.mult, ALU.add),
         reads=("ct0", "ct1"), writes=("ct3",))
    P.op("dve", lambda e: e.scalar_tensor_tensor(sm_, y3, 1.0 / 3.0, sm_, ALU.mult, ALU.add),
         reads=("ct2", "ct3"), writes=("ct3",))
    P.op("dve", lambda e: e.tensor_tensor(y2, y2, y2, ALU.mult), reads=("ct1",), writes=("ct1",))
    P.op("dve", lambda e: e.scalar_tensor_tensor(sm_, y2, -0.25, sm_, ALU.mult, ALU.add),
         reads=("ct1", "ct3"), writes=("ct3",))
    P.op("dve", lambda e: e.tensor_scalar(negb[:], vec[:, V_RGB:V_RGB + 16], -1.0, None, ALU.mult), reads=("vec",), writes=("negb",))
    P.op("dve", lambda e: e.tensor_scalar(c8[:, 0:8], sm_, -8.0, None, ALU.mult), reads=("ct3",), writes=("c8a",))
    P.op("dve", lambda e: e.tensor_scalar(c8[:, 8:16], sm_, -16.0, None, ALU.mult), reads=("ct3",), writes=("c8b",))

    def load_x(xT, T, xb, dma=True, conv=True):
        for dc in range(NDC):
            if dma:
                P.dma("sp", [lambda e, dc=dc: e.dma_start(out=acc[:, dc, 0:T], in_=xT[dc * 128:(dc + 1) * 128, :])],
                      "xl%d" % dc, writes=(("acc", dc),))
            if not conv:
                continue
            P.op("act", lambda e, dc=dc: e.activation(xb[:, dc, 0:T], acc[:, dc, 0:T], AF.Identity),
                 reads=(("acc", dc),), writes=(("xb", dc),))
            P.op("dve", lambda e, dc=dc: e.tensor_scalar(acc[:, dc, 0:T], acc[:, dc, 0:T], ALPHA, None, ALU.mult),
                 reads=(("acc", dc),), writes=(("acc", dc),))

    def ffn(tiles, wg, wu, wd, xb, hg, sgt):
        def gu(g):
            sg, kg = load_slab(wcols(wg, g * 256, 256), 16, 256)
            su, ku = load_slab(wcols(wu, g * 256, 256), 16, 256)
            sd, kd = load_slab(wrows(wd, g * 256, 256), 2, D)
            blocks = []
            for fc in range(2):
                for ti, (t0, tw) in enumerate(tiles):
                    def blk(fc=fc, ti=ti, t0=t0, tw=tw):
                        pg, kpg = psum()
                        pu, kpu = psum()
                        for (pp, kpp, ss, kss) in ((pg, kpg, sg, kg), (pu, kpu, su, ku)):
                            for kc in range(16):
                                P.op("pe", lambda e, pp=pp, ss=ss, kc=kc: e.matmul(
                                    pp[:, 0:tw], ss[:, kc, fc * 128:(fc + 1) * 128], xb[:, kc, t0:t0 + tw],
                                    start=(kc == 0), stop=(kc == 15)),
                                    reads=(kss, ("xb", kc)), writes=(kpp,))
                        st = sgt[(fc * 3 + ti) % 2]
                        kst = ("sgt", (fc * 3 + ti) % 2)
                        P.op("act", lambda e: e.activation(st[:, 0:tw], pg[:, 0:tw], AF.Silu),
                             reads=(kpg,), writes=(kst,))
                        P.op("dve", lambda e: e.tensor_tensor(
                            hg[g % 2][:, fc, t0:t0 + tw], pu[:, 0:tw], st[:, 0:tw], ALU.mult),
                            reads=(kpu, kst), writes=(("hg", g % 2, fc, ti),))
                    blocks.append(blk)
            return blocks, sd, kd

        def down(g, sd, kd):
            blocks = []
            for dc in range(NDC):
                for ti, (t0, tw) in enumerate(tiles):
                    def blk(dc=dc, ti=ti, t0=t0, tw=tw):
                        pd, kpd = psum()
                        for fc in range(2):
                            P.op("pe", lambda e, fc=fc: e.matmul(
                                pd[:, 0:tw], sd[:, fc, dc * 128:(dc + 1) * 128], hg[g % 2][:, fc, t0:t0 + tw],
                                start=(fc == 0), stop=(fc == 1)),
                                reads=(kd, ("hg", g % 2, fc, ti)), writes=(kpd,))
                        P.op("dve", lambda e: e.scalar_tensor_tensor(
                            acc[:, dc, t0:t0 + tw], pd[:, 0:tw], 0.5, acc[:, dc, t0:t0 + tw], ALU.mult, ALU.add),
                            reads=(kpd, ("acc", dc)), writes=(("acc", dc),))
                    blocks.append(blk)
            return blocks

        prev = None
        for g in range(NG + 1):
            gub, cur = [], None
            if g < NG:
                gub, sd_, kd_ = gu(g)
                cur = (sd_, kd_)
            dnb = down(g - 1, *prev) if prev is not None else []
            if gub:
                r = (len(dnb) + len(gub) - 1) // len(gub) if dnb else 0
                for i, gb in enumerate(gub):
                    gb()
                    for db in dnb[i * r:(i + 1) * r]:
                        db()
                for db in dnb[len(gub) * r:]:
                    db()
            else:
                for db in dnb:
                    db()
            prev = cur

    def layernorm(tiles, li, xb, lnt, final=False, ost=None, need_acc=True):
        gcol = V_LN + li * 32
        bcol = gcol + 16
        sq, means, rstds, tt = lnt

        def stats(ti, t0, tw):
            mean, rstd = means[ti], rstds[ti]
            km, kr_ = ("lnmean", ti), ("lnrstd", ti)
            p1, k1 = psum()
            p2, k2 = psum()
            for dc in range(NDC):
                P.op("pe", lambda e, p1=p1, dc=dc: e.matmul(
                    p1[:, 0:tw], ones[:], acc[:, dc, t0:t0 + tw], start=(dc == 0), stop=(dc == 15)),
                    reads=("ones", ("acc", dc)), writes=(k1,))
            for dc in range(NDC):
                s_ = sq[dc % 2]
                P.op("act", lambda e, s_=s_, dc=dc: e.activation(
                    s_[:, 0:tw], acc[:, dc, t0:t0 + tw], AF.Square), reads=(("acc", dc),), writes=(("lnsq", dc % 2),))
                P.op("pe", lambda e, p2=p2, s_=s_, dc=dc: e.matmul(
                    p2[:, 0:tw], onesb[:], s_[:, 0:tw], start=(dc == 0), stop=(dc == 15)),
                    reads=("onesb", ("lnsq", dc % 2)), writes=(k2,))
            P.op("dve", lambda e, p1=p1: e.tensor_scalar(mean[:, 0:tw], p1[:, 0:tw], 1.0 / D, None, ALU.mult),
                 reads=(k1,), writes=(km,))
            P.op("dve", lambda e: e.tensor_tensor(rstd[:, 0:tw], mean[:, 0:tw], mean[:, 0:tw], ALU.mult),
                 reads=(km,), writes=(kr_,))
            P.op("dve", lambda e, p2=p2: e.scalar_tensor_tensor(
                rstd[:, 0:tw], p2[:, 0:tw], 1.0 / D, rstd[:, 0:tw], ALU.mult, ALU.subtract),
                reads=(k2, kr_), writes=(kr_,))
            P.op("dve", lambda e: e.tensor_scalar(rstd[:, 0:tw], rstd[:, 0:tw], 0.0, LN_EPS, ALU.max, ALU.add),
                 reads=(kr_,), writes=(kr_,))
            P.op("act", lambda e: e.activation(rstd[:, 0:tw], rstd[:, 0:tw], AF.Sqrt),
                 reads=(kr_,), writes=(kr_,))
            P.op("dve", lambda e: e.reciprocal(rstd[:, 0:tw], rstd[:, 0:tw]),
                 reads=(kr_,), writes=(kr_,))

        def norm(ti, t0, tw):
            mean, rstd = means[ti], rstds[ti]
            km, kr_ = ("lnmean", ti), ("lnrstd", ti)
            for dc in range(NDC):
                t = tt[dc % 2]
                kt = ("lnt", dc % 2)
                P.op("dve", lambda e, t=t, dc=dc: e.tensor_tensor(
                    t[:, 0:tw], acc[:, dc, t0:t0 + tw], mean[:, 0:tw], ALU.subtract),
                    reads=(("acc", dc), km), writes=(kt,))
                P.op("dve", lambda e, t=t: e.tensor_tensor(t[:, 0:tw], t[:, 0:tw], rstd[:, 0:tw], ALU.mult),
                     reads=(kt, kr_), writes=(kt,))
                if not final:
                    P.op("act", lambda e, t=t, dc=dc: e.activation(
                        xb[:, dc, t0:t0 + tw], t[:, 0:tw], AF.Identity,
                        bias=vec[:, bcol + dc:bcol + dc + 1], scale=vec[:, gcol + dc:gcol + dc + 1]),
                        reads=(kt, "vec"), writes=(("xb", dc),))
                    if not need_acc:
                        pass
                    elif dc % 2 == 0:
                        P.op("act", lambda e, t=t, dc=dc: e.activation(
                            acc[:, dc, t0:t0 + tw], t[:, 0:tw], AF.Identity,
                            bias=lnsc[:, li * 32 + 16 + dc:li * 32 + 17 + dc], scale=lnsc[:, li * 32 + dc:li * 32 + dc + 1]),
                            reads=(kt, "lnsc"), writes=(("acc", dc, ti),))
                    else:
                        P.op("dve", lambda e, t=t, dc=dc: e.tensor_scalar(
                            acc[:, dc, t0:t0 + tw], t[:, 0:tw], lnsc[:, li * 32 + dc:li * 32 + dc + 1],
                            lnsc[:, li * 32 + 16 + dc:li * 32 + 17 + dc], ALU.mult, ALU.add),
                            reads=(kt, "lnsc"), writes=(("acc", dc, ti),))
                else:
                    P.op("act", lambda e, t=t, dc=dc: e.activation(
                        acc[:, dc, t0:t0 + tw], t[:, 0:tw], AF.Identity,
                        bias=vec[:, bcol + dc:bcol + dc + 1], scale=vec[:, gcol + dc:gcol + dc + 1]),
                        reads=(kt, "vec"), writes=(("acc", dc, ti),))
                    P.dma("sp", [lambda e, dc=dc: e.dma_start(
                        out=yT[dc * 128:(dc + 1) * 128, t0:t0 + tw], in_=acc[:, dc, t0:t0 + tw])],
                        "yo%d" % (dc % 4), reads=(("acc", dc, ti),))

        n = len(tiles)
        stats(0, *tiles[0])
        for ti in range(n):
            if ti + 1 < n:
                stats(ti + 1, *tiles[ti + 1])
            norm(ti, *tiles[ti])

    def mixer_tile(xs, xkey, t0g, cs_off, tw, nseq, L, sidx0, full, R, cs_t, ya, yb, sw_of_seq, tile_id):
        nblk = tw // 128
        w_in = W["w_in"]
        P.dma("sp", [lambda e: e.dma_start(out=cs_t[:, :, 0:tw], in_=csd[:, :, cs_off:cs_off + tw])],
              "cs", writes=("cs",))

        def rf(off, n):
            return R[:, off:off + n]

        def rb(off_f32, n):
            return R[:, off_f32:off_f32 + (n + 1) // 2].bitcast(BF16)[:, 0:n]

        def proj(slab, kslab, col0, pp, kpp):
            for kc in range(16):
                P.op("pe", lambda e, kc=kc: e.matmul(pp[:, 0:tw], slab[:, kc, col0:col0 + 128], xs(kc),
                                                     start=(kc == 0), stop=(kc == 15)),
                     reads=(kslab, xkey), writes=(kpp,))


        XP, XC, RR, II, AA, A2, UU, HH, GS, T1, SG, XCB = [i * 528 for i in range(12)]
        tws = tw // 2
        if nseq == 1:
            Ls, ns = tws, 1
        else:
            Ls, ns = L, nseq // 2
        WPs = Ls + 3
        K2 = 2.0 * math.sqrt(2.0 / math.pi)
        lru_slabs = {}

        def job(c, sub):
            so = sub * tws
            sidx = sidx0 if nseq == 1 else sidx0 + sub * ns
            bo_ = sub * 264

            def f(off, n=tws):
                return rf(off + bo_, n)

            def f3(off):
                return f(off).rearrange("p (s l) -> p s l", l=Ls)
            xp3 = rf(XP + bo_, ns * WPs).rearrange("p (s l) -> p s l", l=WPs)
            xc3 = f3(XC)
            xcb = rb(XCB + sub * 132, tws)
            k = lambda nm: (nm, sub)
            col0 = (c % 2) * 128
            sxa, kxa = lru_slabs[("xa", c // 2)]
            cw = lambda j: vec[:, V_CW + c * 4 + j:V_CW + c * 4 + j + 1]
            xsl = lambda kc: xs(kc)[:, so:so + tws]
            st = []
            pxa, kpxa = psum()
            S = []
            for kc in range(16):
                S.append(("pe", lambda e, kc=kc: e.matmul(pxa[:, 0:tws], sxa[:, kc, col0:col0 + 128], xsl(kc),
                                                          start=(kc == 0), stop=(kc == 15)), (kxa, xkey), (kpxa,)))
            st.append(S)
            st.append([
                ("act", lambda e: e.activation(xp3[:, :, 3:WPs], pxa[:, 0:tws].rearrange("p (s l) -> p s l", l=Ls), AF.Identity),
                 (kpxa,), (k("xp"),)),
                ("dve", lambda e: e.tensor_copy(xp3[:, :, 0:3], cst[:, c, sidx:sidx + ns, :]), ("cst",), (k("xp"),)),
            ])
            st.append([("act", lambda e: e.activation(xc3, xp3[:, :, 0:Ls], AF.Identity, bias=vec[:, V_CB + c:V_CB + c + 1], scale=cw(0)),
                        (k("xp"), "vec"), (k("xc"),))])
            S = []
            for j in range(1, 4):
                S.append(("dve", lambda e, j=j: e.scalar_tensor_tensor(xc3, xp3[:, :, j:j + Ls], cw(j), xc3, ALU.mult, ALU.add),
                          (k("xp"), k("xc"), "vec"), (k("xc"),)))
            S.append(("dve", lambda e: e.tensor_copy(cst[:, c, sidx:sidx + ns, :], xp3[:, :, Ls:Ls + 3]), (k("xp"),), ("cst",)))
            st.append(S)
            st.append([("act", lambda e: e.activation(xcb, f(XC), AF.Identity), (k("xc"),), (k("xcb"),))])
            pr, kpr = psum()
            pi, kpi = psum()
            S = [("pe", lambda e: e.matmul(pr[:, 0:tws], bd[:, 2 * c, :], xcb, start=True, stop=True), ("bd", k("xcb")), (kpr,)),
                 ("pe", lambda e: e.matmul(pi[:, 0:tws], bd[:, 2 * c + 1, :], xcb, start=True, stop=True), ("bd", k("xcb")), (kpi,))]
            if full:
                sga, kga = lru_slabs[("ga", c // 2)]
                pga, kpga = psum()
                for kc in range(16):
                    S.append(("pe", lambda e, kc=kc: e.matmul(pga[:, 0:tws], sga[:, kc, col0:col0 + 128], xsl(kc),
                                                              start=(kc == 0), stop=(kc == 15)), (kga, xkey), (kpga,)))
            st.append(S)
            gl = ((pr, kpr, RR, "rr", 0), (pi, kpi, II, "ii", 8))
            S = []
            for (pz, kpz, OFF, kk, bo) in gl:
                S.append(("act", lambda e, pz=pz, OFF=OFF, bo=bo: e.activation(f(OFF), pz[:, 0:tws], AF.Exp,
                                                                              bias=negb[:, bo + c:bo + c + 1], scale=-1.0),
                          (kpz, "negb"), (k(kk),)))
            for (pz, kpz, OFF, kk, bo) in gl:
                S.append(("act", lambda e, OFF=OFF: e.activation(f(OFF), f(OFF), AF.Ln, bias=ones[:, 0:1]), (k(kk), "ones"), (k(kk),)))
            for (pz, kpz, OFF, kk, bo) in gl:
                S.append(("act", lambda e, OFF=OFF: e.activation(f(OFF), f(OFF), AF.Exp, scale=-1.0), (k(kk),), (k(kk),)))
            S.append(("act", lambda e: e.activation(f(AA), f(RR), AF.Exp, scale=c8[:, c:c + 1]), (k("rr"), "c8a"), (k("aa"),)))
            S.append(("act", lambda e: e.activation(f(A2), f(RR), AF.Identity, scale=c8[:, 8 + c:9 + c]), (k("rr"), "c8b"), (k("a2"),)))
            S.append(("act", lambda e: e.activation(f(UU), f(A2), AF.Identity, scale=0.2), (k("a2"),), (k("uu"),)))
            if full:
                S.append(("act", lambda e: e.activation(f(GS), pga[:, 0:tws], AF.Identity), (kpga,), (k("gs"),)))
                S.append(("act", lambda e: e.activation(f(T1), pga[:, 0:tws], AF.Square, scale=math.sqrt(0.044715)), (kpga,), (k("t1"),)))
            st.append(S)
            S = []
            for cst_ in (1.0, 4.0, 12.0, 24.0):
                S.append(("dve", lambda e, cst_=cst_: e.scalar_tensor_tensor(f(UU), f(UU), cst_, f(A2), ALU.add, ALU.mult),
                          (k("uu"), k("a2")), (k("uu"),)))
            if full:
                S.append(("dve", lambda e: e.scalar_tensor_tensor(f(T1), f(T1), 1.0, f(GS), ALU.add, ALU.mult),
                          (k("t1"), k("gs")), (k("t1"),)))
            st.append(S)
            S = [("act", lambda e: e.activation(f(UU), f(UU), AF.Ln, scale=-1.0 / 24.0), (k("uu"),), (k("uu"),)),
                 ("act", lambda e: e.activation(f(A2), f(UU), AF.Exp, scale=0.5), (k("uu"),), (k("a2"),))]
            if full:
                S.append(("act", lambda e: e.activation(f(SG), f(T1), AF.Exp, scale=-K2), (k("t1"),), (k("sg"),)))
                S.append(("act", lambda e: e.activation(f(SG), f(SG), AF.Ln, bias=ones[:, 0:1]), (k("sg"), "ones"), (k("sg"),)))
                S.append(("act", lambda e: e.activation(f(SG), f(SG), AF.Exp, scale=-1.0), (k("sg"),), (k("sg"),)))
            st.append(S)
            S = [("dve", lambda e: e.tensor_tensor(f(UU), f(A2), f(II), ALU.mult), (k("a2"), k("ii")), (k("uu"),)),
                 ("dve", lambda e: e.tensor_tensor(f(UU), f(UU), f(XC), ALU.mult), (k("uu"), k("xc")), (k("uu"),)),
                 ("dve", lambda e: e.tensor_tensor(f(T1 if not full else HH, ns), f3(AA)[:, :, 0], hst[:, c, sidx:sidx + ns], ALU.mult),
                  (k("aa"), "hst"), (k("hh"),)),
                 ("dve", lambda e: e.tensor_tensor(f3(UU)[:, :, 0], f3(UU)[:, :, 0], f(T1 if not full else HH, ns), ALU.add),
                  (k("uu"), k("hh")), (k("uu"),)),
                 ("dve", lambda e: e.memset(f3(AA)[:, :, 0], 0.0), (k("hh"),), (k("aa"),)),
                 ("dve", lambda e: e.tensor_tensor_scan(f(HH), f(AA), f(UU), 0.0, ALU.mult, ALU.add), (k("aa"), k("uu")), (k("hh"),)),
                 ("dve", lambda e: e.tensor_copy(hst[:, c, sidx:sidx + ns], f3(HH)[:, :, Ls - 1]), (k("hh"),), ("hst",))]
            if full:
                S.append(("dve", lambda e: e.tensor_tensor(f(GS), f(GS), f(SG), ALU.mult), (k("gs"), k("sg")), (k("gs"),)))
                S.append(("dve", lambda e: e.tensor_tensor(ya[:, c, so:so + tws], f(HH), f(GS), ALU.mult), (k("hh"), k("gs")), (("ya", c),)))
            st.append(S)
            return st

        jobs = [(c, sub) for c in range(8) for sub in range(2)]
        built = {}
        nst = None
        for step in range(len(jobs) * 5 + 16):
            for i, (c, sub) in enumerate(jobs):
                kst = step - (i // 2) * 10 - 2 * (i % 2)
                if kst < 0 or kst > 9:
                    continue
                if i not in built:
                    if sub == 0 and c % 2 == 0:
                        lru_slabs[("xa", c // 2)] = load_slab(wcols(w_in, c * 128, 256), 16, 256)
                        if full:
                            lru_slabs[("ga", c // 2)] = load_slab(wcols(w_in, 1024 + c * 128, 256), 16, 256)
                    built[i] = job(c, sub)
                for (eng, fn, rds, wrs) in built[i][kst]:
                    P.op(eng, fn, reads=rds, writes=wrs)

        P.barrier()
        P.cp(5)
        VT = 0
        o = nblk * 512
        QR, QD, KR = o, o + 256, o + 512
        KT = o + 768
        QB = o + 1024
        TA, TB = o + 1280, o + 1792
        OT = o + 2304
        SM = o + 2816
        ME, RS = o + 2944, o + 3456
        vt = rb(VT, nblk * 1024).rearrange("p (b f) -> p b f", f=1024)
        qr, qd, kr, qb = rb(QR, tw), rb(QD, tw), rb(KR, tw), rb(QB, tw)
        kt = rb(KT, nblk * 128).rearrange("p (b f) -> p b f", f=128)
        smb = rb(SM, 256).rearrange("p (b f) -> p b f", f=128)
        ta, tb, ot, me, rs = rf(TA, tw), rf(TB, tw), rf(OT, tw), rf(ME, tw), rf(RS, tw)
        SALLB = o + 3968
        sallb = rb(SALLB, (tw // 64) * 128).rearrange("p (c f) -> p c f", f=128)
        SM2 = o + 4480
        smb2 = rb(SM2, 256).rearrange("p (b f) -> p b f", f=128)
        smb4 = [smb[:, 0, :], smb[:, 1, :], smb2[:, 0, :], smb2[:, 1, :]]
        assert SM2 + 128 <= RSZ
        cosv, sinv = cs_t[:, 0, 0:tw], cs_t[:, 1, 0:tw]
        for sl in range(4):
            sv, kv = load_slab(wcols(w_in, 4096 + sl * 256, 256), 16, 256)
            for b in range(nblk):
                pv, kpv = psum()
                for kc in range(16):
                    P.op("pe", lambda e, pv=pv, kc=kc, b=b, sv=sv: e.matmul(
                        pv[:, 0:256], xs(kc)[:, b * 128:(b + 1) * 128], sv[:, kc, :], start=(kc == 0), stop=(kc == 15)),
                        reads=(kv, xkey), writes=(kpv,))
                eng = "dve"
                if eng == "act":
                    P.op("act", lambda e, pv=pv, b=b, sl=sl: e.activation(vt[:, b, sl * 256:(sl + 1) * 256], pv[:, 0:256], AF.Identity),
                         reads=(kpv,), writes=(("vt", b, sl),))
                else:
                    P.op("dve", lambda e, pv=pv, b=b, sl=sl: e.tensor_copy(vt[:, b, sl * 256:(sl + 1) * 256], pv[:, 0:256]),
                         reads=(kpv,), writes=(("vt", b, sl),))

        P.cp(5.1)

        def rope(slab, kslab, col0, dst, kdst):
            pq, kpq = psum()
            proj(slab, kslab, col0, pq, kpq)
            P.op("dve", lambda e: e.tensor_copy(qb, pq[:, 0:tw]), reads=(kpq,), writes=("qb",))
            pp, kpp = psum()
            P.op("pe", lambda e: e.matmul(pp[:, 0:tw], perm[:], qb, start=True, stop=True),
                 reads=("perm", "qb"), writes=(kpp,))
            P.op("dve", lambda e: e.tensor_tensor(ta, pq[:, 0:tw], cosv, ALU.mult), reads=(kpq, "cs"), writes=("ta",))
            P.op("dve", lambda e: e.tensor_tensor(tb, pp[:, 0:tw], sinv, ALU.mult), reads=(kpp, "cs"), writes=("tb",))
            P.op("dve", lambda e: e.tensor_tensor(dst, ta, tb, ALU.add), reads=("ta", "tb"), writes=(kdst,))

        nch = L // 64
        prevC = [None]
        for hp in range(4):
            if full:
                sq_, ksq = load_slab(wcols(w_in, 2048 + hp * 256, 256), 16, 256)
            sk_, ksk = load_slab(wcols(w_in, 3072 + hp * 256, 256), 16, 256)
            if full:
                sg_, ksg = load_slab(wcols(w_in, 5120 + hp * 256, 256), 16, 256)
            for hh in range(2):
                hd = hp * 2 + hh
                col0 = hh * 128
                P.cap = []
                if full:
                    rope(sq_, ksq, col0, qr, "qr")
                    P.op("dve", lambda e, hd=hd: e.tensor_tensor(
                        qd.rearrange("p (c i) -> p c i", i=64), qr.rearrange("p (c i) -> p c i", i=64),
                        qdec[:, hd:hd + 1, :].broadcast_to([128, tw // 64, 64]), ALU.mult),
                        reads=("qr", "qdec"), writes=("qd",))
                rope(sk_, ksk, col0, kr, "kr")
                P.cp(5.2)
                for b in range(nblk):
                    pk, kpk = psum()
                    P.op("pe", lambda e, pk=pk, b=b: e.matmul(pk[:, 0:128], kr[:, b * 128:(b + 1) * 128], ident[:],
                                                              start=True, stop=True),
                         reads=("kr", "ident"), writes=(kpk,))
                    P.op("dve", lambda e, pk=pk, b=b, hd=hd: e.tensor_scalar(kt[:, b, :], pk[:, 0:128], kdecP[:, hd:hd + 1], None, ALU.mult),
                         reads=(kpk, "kdecP"), writes=(("kt", b),))
                P.cp(5.3)
                nchT = tw // 64
                pkv = []
                kvb = [psum(), psum()]
                for ch in range(nchT):
                    bank, kbank = kvb[ch % 2]
                    pkv.append((bank[:, (ch // 2) * 128:(ch // 2 + 1) * 128], kbank))
                    b = ch // 2
                    lo = (ch % 2) * 64
                    P.op("pe", lambda e, dstp=pkv[ch][0], b=b, lo=lo, hd=hd: e.matmul(
                        dstp, kt[lo:lo + 64, b, :], vt[lo:lo + 64, b, hd * 128:(hd + 1) * 128], start=True, stop=True,
                        skip_group_check=True),
                        reads=(("kt", b),) + tuple(("vt", b, s_) for s_ in range(4)), writes=(kbank,))
                for ch in range(nchT):
                    j = (ch * 64) // L
                    swt, ksw0 = sw_of_seq(j)
                    ksw = (ksw0, hd)
                    if full and (ch * 64) % L == 0:
                        P.op("dve", lambda e, swt=swt, hd=hd, ch=ch: e.tensor_copy(sallb[:, ch, :], swt[:, hd, :]),
                             reads=(ksw, ksw0), writes=(("sallb", ch),))
                    P.op("dve", lambda e, srcp=pkv[ch][0], swt=swt, hd=hd: e.scalar_tensor_tensor(
                        swt[:, hd, :], swt[:, hd, :], G64[hd], srcp, ALU.mult, ALU.add),
                        reads=(pkv[ch][1], ksw, ksw0), writes=(ksw,))
                    if full and ch + 1 < nchT and ((ch + 1) * 64) // L == j:
                        P.op("dve", lambda e, swt=swt, hd=hd, ch=ch: e.tensor_copy(sallb[:, ch + 1, :], swt[:, hd, :]),
                             reads=(ksw,), writes=(("sallb", ch + 1),))
                partA, P.cap = P.cap, None
                partC = prevC[0] or []
                prevC[0] = None
                for i_ in range(max(len(partA), len(partC))):
                    if i_ < len(partC):
                        P.op(*partC[i_][:2], reads=partC[i_][2], writes=partC[i_][3])
                    if i_ < len(partA):
                        P.op(*partA[i_][:2], reads=partA[i_][2], writes=partA[i_][3])
                if full:
                    psl = []
                    for b in range(nblk):
                        ps_, kps = psum()
                        psl.append((ps_, kps))
                        P.op("pe", lambda e, ps_=ps_, b=b: e.matmul(ps_[:, 0:128], kr[:, b * 128:(b + 1) * 128],
                                                                    qr[:, b * 128:(b + 1) * 128], start=True, stop=True),
                             reads=("kr", "qr"), writes=(kps,))
                    for b in range(nblk):
                        ps_, kps = psl[b]
                        P.op("dve", lambda e, ps_=ps_, b=b, hd=hd: e.tensor_tensor(smb4[b], ps_[:, 0:128], dmT[:, hd, :], ALU.mult),
                             reads=(kps, "dmT"), writes=(("smb", b),))
                    pol = []
                    for b in range(nblk):
                        po, kpo = psum()
                        pol.append((po, kpo))
                        P.op("pe", lambda e, po=po, b=b, hd=hd: e.matmul(
                            po[:, 0:128], vt[:, b, hd * 128:(hd + 1) * 128], smb4[b], start=True, stop=False, skip_group_check=True),
                            reads=(("smb", b),) + tuple(("vt", b, s_) for s_ in range(4)), writes=(kpo,))
                        for ch2 in range(2):
                            ch = 2 * b + ch2
                            P.op("pe", lambda e, po=po, ch=ch, ch2=ch2: e.matmul(
                                po[:, ch2 * 64:(ch2 + 1) * 64], sallb[:, ch, :], qd[:, ch * 64:(ch + 1) * 64], start=False, stop=(ch2 == 1),
                                skip_group_check=True),
                                reads=(("sallb", ch), "qd"), writes=(kpo,))
                    for b in range(nblk):
                        po, kpo = pol[b]
                        P.op("act", lambda e, po=po, b=b: e.activation(ot[:, b * 128:(b + 1) * 128], po[:, 0:128], AF.Identity),
                             reads=(kpo,), writes=("ot",))
                if full:
                    P.cap = []
                    pm, kpm = psum()
                    pq2, kpq2 = psum()
                    P.op("pe", lambda e, pm=pm: e.matmul(pm[:, 0:tw], ones[:], ot, start=True, stop=True),
                         reads=("ones", "ot"), writes=(kpm,))
                    P.op("act", lambda e: e.activation(rs, ot, AF.Square), reads=("ot",), writes=("rs",))
                    P.op("pe", lambda e, pq2=pq2: e.matmul(pq2[:, 0:tw], ones[:], rs, start=True, stop=True),
                         reads=("ones", "rs"), writes=(kpq2,))
                    P.op("dve", lambda e, pm=pm: e.tensor_scalar(me, pm[:, 0:tw], 1.0 / 128, None, ALU.mult), reads=(kpm,), writes=("me",))
                    P.op("dve", lambda e: e.tensor_tensor(rs, me, me, ALU.mult), reads=("me", kpq2), writes=("rs",))
                    P.op("dve", lambda e, pq2=pq2: e.scalar_tensor_tensor(rs, pq2[:, 0:tw], 1.0 / 128, rs, ALU.mult, ALU.subtract),
                         reads=(kpq2, "rs"), writes=("rs",))
                    P.op("dve", lambda e: e.tensor_scalar(rs, rs, 0.0, 1e-5, ALU.max, ALU.add), reads=("rs",), writes=("rs",))
                    P.op("act", lambda e: e.activation(rs, rs, AF.Ln), reads=("rs",), writes=("rs",))
                    P.op("act", lambda e: e.activation(rs, rs, AF.Exp, scale=-0.5), reads=("rs",), writes=("rs",))
                    P.op("dve", lambda e: e.tensor_tensor(ot, ot, me, ALU.subtract), reads=("ot", "me"), writes=("ot",))
                    P.op("dve", lambda e: e.tensor_tensor(ot, ot, rs, ALU.mult), reads=("ot", "rs"), writes=("ot",))
                    P.op("act", lambda e, hd=hd: e.activation(ot, ot, AF.Identity, bias=vec[:, V_GNB + hd:V_GNB + hd + 1],
                                                               scale=vec[:, V_GNG + hd:V_GNG + hd + 1]),
                         reads=("ot", "vec"), writes=("ot",))
                    pg, kpg = psum()
                    proj(sg_, ksg, col0, pg, kpg)
                    P.op("act", lambda e, pg=pg: e.activation(me, pg[:, 0:tw], AF.Exp, scale=-1.0), reads=(kpg, "ot"), writes=("me",))
                    P.op("act", lambda e: e.activation(me, me, AF.Ln, bias=ones[:, 0:1]), reads=("me", "ones"), writes=("me",))
                    P.op("act", lambda e: e.activation(me, me, AF.Exp, scale=-1.0), reads=("me",), writes=("me",))
                    P.op("dve", lambda e, pg=pg: e.tensor_tensor(me, me, pg[:, 0:tw], ALU.mult), reads=("me", kpg), writes=("me",))
                    P.op("dve", lambda e, hd=hd: e.tensor_tensor(yb[:, hd, 0:tw], ot, me, ALU.mult),
                         reads=("me", "ot"), writes=(("yb", hd),))
                    prevC[0], P.cap = P.cap, None
        for it_ in (prevC[0] or []):
            P.op(*it_[:2], reads=it_[2], writes=it_[3])
        prevC[0] = None
        P.barrier()
        P.cp(6)
        if not full:
            return
        merged = rb(0, 16 * tw).rearrange("p (c t) -> p c t", t=tw)
        SA, SB_, M1 = 4096, 4608, 5120
        sa, sb_, m1 = rf(SA, tw), rf(SB_, tw), rf(M1, tw)
        for sp in range(8):
            sga_, kga_ = load_slab(wcols(w_in, 6144 + sp * 256, 256), 16, 256)
            sgb_, kgb_ = load_slab(wcols(w_in, 8192 + sp * 256, 256), 16, 256)
            swa, kwa = load_slab(wcols(W["w_a"], sp * 256, 256), 8, 256)
            swb, kwb = load_slab(wcols(W["w_b"], sp * 256, 256), 8, 256)
            for cc in range(2):
                dc = sp * 2 + cc
                col0 = cc * 128
                pa, kpa = psum()
                proj(sga_, kga_, col0, pa, kpa)
                pb, kpb = psum()
                proj(sgb_, kgb_, col0, pb, kpb)
                pya, kpya = psum()
                for c in range(8):
                    P.op("pe", lambda e, c=c, pya=pya, swa=swa, col0=col0: e.matmul(pya[:, 0:tw], swa[:, c, col0:col0 + 128], ya[:, c, 0:tw],
                                                                start=(c == 0), stop=(c == 7)),
                         reads=(kwa, ("ya", c)), writes=(kpya,))
                pyb, kpyb = psum()
                for c in range(8):
                    P.op("pe", lambda e, c=c, pyb=pyb, swb=swb, col0=col0: e.matmul(pyb[:, 0:tw], swb[:, c, col0:col0 + 128], yb[:, c, 0:tw],
                                                                start=(c == 0), stop=(c == 7)),
                         reads=(kwb, ("yb", c)), writes=(kpyb,))
                P.op("act", lambda e, pa=pa: e.activation(sa, pa[:, 0:tw], AF.Sigmoid), reads=(kpa,), writes=("sa",))
                P.op("act", lambda e, pb=pb: e.activation(sb_, pb[:, 0:tw], AF.Sigmoid), reads=(kpb,), writes=("sb",))
                P.op("dve", lambda e, pya=pya: e.tensor_tensor(m1, sa, pya[:, 0:tw], ALU.mult), reads=("sa", kpya), writes=("m1",))
                P.op("dve", lambda e, pyb=pyb: e.tensor_tensor(sb_, sb_, pyb[:, 0:tw], ALU.mult), reads=("sb", kpyb), writes=("sb",))
                P.op("dve", lambda e, dc=dc: e.tensor_tensor(merged[:, dc, :], m1, sb_, ALU.add), reads=("m1", "sb"),
                     writes=(("mg", dc),))
        for sp in range(8):
            swo, kwo = load_slab(wcols(W["w_o"], sp * 256, 256), 16, 256)
            for cc in range(2):
                dc2 = sp * 2 + cc
                pw, kpw = psum()
                for dc in range(16):
                    P.op("pe", lambda e, dc=dc, pw=pw, cc=cc, swo=swo: e.matmul(pw[:, 0:tw], swo[:, dc, cc * 128:(cc + 1) * 128], merged[:, dc, :],
                                                                        start=(dc == 0), stop=(dc == 15)),
                         reads=(kwo, ("mg", dc)), writes=(kpw,))
                P.op("dve", lambda e, pw=pw, dc2=dc2: e.tensor_tensor(acc[:, dc2, t0g:t0g + tw], acc[:, dc2, t0g:t0g + tw], pw[:, 0:tw], ALU.add),
                     reads=(kpw, ("acc", dc2)), writes=(("acc", dc2),))
        if tile_id == 0 and tw == 512:
            for c in range(8):
                dump(c, ya[:, c, 0:512], [("ya", c)])
                dump(8 + c, yb[:, c, 0:512], [("yb", c)])
            for dc in range(16):
                dump(16 + dc, merged[:, dc, :], [("mg", dc)])
                dump(32 + dc, acc[:, dc, t0g:t0g + 512], [("acc", dc)])
                dump(48 + dc, xs(dc), [xkey])
        P.barrier()

    try:
        P.cp(1)
        def ffn_phase(xT, T, tiles, wg, wu, wd, li, xb=None, skip_load=False, final=False, x_prefetched=False, need_acc=True):
            with ExitStack() as ph:
                def sbp(name, shape, dt=F32):
                    return ph.enter_context(sbt(name, list(shape), dt))
                if xb is None:
                    xb = sbp("xb", [128, NDC, TM], BF16)
                with ExitStack() as ph2:
                    slabs["t"] = [ph2.enter_context(sbt("slab%d" % i, [128, 4096], BF16)) for i in range(6)]
                    slabs["i"] = 0
                    hg = [ph2.enter_context(sbt("hg%d" % i, [128, 2, TM], BF16)) for i in range(2)]
                    sgt = [ph2.enter_context(sbt("sgt%d" % i, [128, 512], F32)) for i in range(2)]
                    if not skip_load:
                        load_x(xT, T, xb, dma=not x_prefetched)
                    else:
                        for dc in range(NDC):
                            if dc % 2 == 0:
                                P.op("act", lambda e, dc=dc: e.activation(xb[:, dc, 0:T], acc[:, dc, 0:T], AF.Identity, scale=1.0 / ALPHA),
                                     reads=(("acc", dc),), writes=(("xb", dc),))
                            else:
                                P.op("dve", lambda e, dc=dc: e.tensor_scalar(xb[:, dc, 0:T], acc[:, dc, 0:T], 1.0 / ALPHA, None, ALU.mult),
                                     reads=(("acc", dc),), writes=(("xb", dc),))
                    P.cp(2)
                    ffn(tiles, wg, wu, wd, xb, hg, sgt)
                    P.cp(3)
                    P.barrier(keep_prefix=("@none@",), skip=())
                with ExitStack() as ph3:
                    def sb3(name, shape):
                        return ph3.enter_context(sbt(name, list(shape), F32))
                    lnt = ([ph3.enter_context(sbt("lnsq%d" % i, [128, 512], BF16)) for i in range(2)], [sb3("lnmean%d" % i, [128, 512]) for i in range(3)],
                           [sb3("lnrstd%d" % i, [128, 512]) for i in range(3)], [sb3("lnt%d" % i, [128, 512]) for i in range(2)])
                    layernorm(tiles, li, xb, lnt, final=final, need_acc=need_acc)
                    P.barrier(keep_prefix=("@none@",), skip=())
                    P.cp(4)

        with ExitStack() as php:
            xbp = php.enter_context(sbt("xbp", [128, NDC, TP], BF16))
            ffn_phase(xpT, TP, TILES_P, W["f1g"], W["f1u"], W["f1d"], 0, xb=xbp, need_acc=False)
            load_x(xmT, TM, None, dma=True, conv=False)
            slabs["t"] = [php.enter_context(sbt("pslab%d" % i, [128, 4096], BF16)) for i in range(4)]
            slabs["i"] = 0
            Rp = php.enter_context(sbt("Rp", [128, RSZ], F32))
            cs_p = php.enter_context(sbt("cstp", [128, 2, 512], F32))
            for ti, (t0, tw) in enumerate(TILES_P):
                mixer_tile(lambda kc, t0=t0, tw=tw: xbp[:, kc, t0:t0 + tw], "xball", t0, TM + t0, tw, 1, 512, 0,
                           False, Rp, cs_p, None, None, lambda j: (Sw[0], ("Sw", 0)), ti)
            P.barrier(keep_prefix=("@none@",), skip=())

        P.op("dve", lambda e: e.tensor_scalar(Sw[0][:].rearrange("p h f -> p (h f)"), Sw[0][:].rearrange("p h f -> p (h f)"),
                                              mask[:, 0:1], None, ALU.mult), reads=(("Sw", 0), "mask") + tuple((("Sw", 0), h_) for h_ in range(H)), writes=(("Sw", 0),) + tuple((("Sw", 0), h_) for h_ in range(H)))
        P.op("dve", lambda e: e.tensor_scalar(hst[:].rearrange("p c s -> p (c s)"), hst[:].rearrange("p c s -> p (c s)"),
                                              mask[:, 0:1], None, ALU.mult), reads=("hst", "mask"), writes=("hst",))
        P.op("dve", lambda e: e.tensor_scalar(cst[:].rearrange("p c s j -> p (c s j)"), cst[:].rearrange("p c s j -> p (c s j)"),
                                              mask[:, 0:1], None, ALU.mult), reads=("cst", "mask"), writes=("cst",))
        P.dma("sp", [lambda e: e.dma_start(out=hst[:, :, 1:5], in_=stlT.rearrange("p (c s) -> p c s", s=4))], "st0",
              reads=(), writes=("hst",))
        P.dma("sp", [lambda e: e.dma_start(out=cst[:, :, 1:5, :], in_=stcT.rearrange("p (c s j) -> p c s j", s=4, j=3))], "st1",
              reads=(), writes=("cst",))

        ffn_phase(xmT, TM, TILES_M, W["f1g"], W["f1u"], W["f1d"], 0, x_prefetched=True)

        with ExitStack() as ph:
            def sbp(name, shape, dt=F32):
                return ph.enter_context(sbt(name, list(shape), dt))
            slabs["t"] = [sbp("mslab%d" % i, [128, 4096], BF16) for i in range(4)]
            slabs["i"] = 0
            R = sbp("R", [128, RSZ])
            cs_t = sbp("cst_", [128, 2, 512])
            xbt = sbp("xbt", [128, NDC, 512], BF16)
            ya = sbp("ya", [128, 8, 512], BF16)
            yb = sbp("yb", [128, 8, 512], BF16)
            for ti, (t0, tw) in enumerate(TILES_M):
                for dc in range(NDC):
                    eng = "act" if dc % 2 == 0 else "dve"
                    if eng == "act":
                        P.op("act", lambda e, dc=dc, t0=t0, tw=tw: e.activation(xbt[:, dc, 0:tw], acc[:, dc, t0:t0 + tw], AF.Identity, scale=1.0 / ALPHA),
                             reads=(("acc", dc),), writes=(("xball", dc),))
                    else:
                        P.op("dve", lambda e, dc=dc, t0=t0, tw=tw: e.tensor_scalar(xbt[:, dc, 0:tw], acc[:, dc, t0:t0 + tw], 1.0 / ALPHA, None, ALU.mult),
                             reads=(("acc", dc),), writes=(("xball", dc),))
                P.barrier()
                if ti < 2:
                    sw_of = lambda j: (Sw[0], ("Sw", 0))
                    mixer_tile(lambda kc, tw=tw: xbt[:, kc, 0:tw], "xball", t0, t0, tw, 1, 512, 0, True, R, cs_t, ya, yb, sw_of, ti)
                    if ti == 1:
                        P.dma("sp", [lambda e: e.dma_start(out=S_o[0].rearrange("h d f -> d h f"), in_=Sw[0][:])], "so0",
                              reads=(("Sw", 0),))
                else:
                    Sx = [Sw[0], sbp("Sw1", [128, H, 128]), sbp("Sw2", [128, H, 128]), sbp("Sw3", [128, H, 128])]
                    for j in range(4):
                        P.dma("sp", [lambda e, j=j: e.dma_start(out=Sx[j][:], in_=strt[j].rearrange("h d f -> d h f"))], "sl%d" % j,
                              reads=(), writes=(("Sw", j),))
                    sw_of = lambda j: (Sx[j], ("Sw", j))
                    mixer_tile(lambda kc, tw=tw: xbt[:, kc, 0:tw], "xball", t0, t0, tw, 4, 64, 1, True, R, cs_t, ya, yb, sw_of, ti)
                    for j in range(4):
                        P.dma("sp", [lambda e, j=j: e.dma_start(out=S_o[1 + j].rearrange("h d f -> d h f"), in_=Sx[j][:])], "so%d" % (1 + j),
                              reads=(("Sw", j),))
            P.dma("sp", [lambda e: e.dma_start(out=hoT[:, :], in_=hst[:].rearrange("p c s -> p (c s)"))], "ho", reads=("hst",))
            P.dma("sp", [lambda e: e.dma_start(out=coT[:, :], in_=cst[:].rearrange("p c s j -> p (c s j)"))], "co", reads=("cst",))
            P.barrier(keep_prefix=("@none@",), skip=())

        with ExitStack() as ph:
            xb2 = ph.enter_context(sbt("xb2", [128, NDC, TM], BF16))
            sq2 = [ph.enter_context(sbt("sq2_%d" % i, [128, 512], BF16)) for i in range(2)]
            lnt = (sq2, [ph.enter_context(sbt("lnm2_%d" % i, [128, 512], F32)) for i in range(3)],
                   [ph.enter_context(sbt("lnr2_%d" % i, [128, 512], F32)) for i in range(3)],
                   [ph.enter_context(sbt("lnt2_%d" % i, [128, 512], F32)) for i in range(2)])
            layernorm(TILES_M, 1, xb2, lnt)
            P.barrier(keep_prefix=("@none@",), skip=())
        ffn_phase(None, TM, TILES_M, W["f2g"], W["f2u"], W["f2d"], 2, skip_load=True, final=True)


    except _Stop:
        pass

    semnames = set()
    for e in ENGS:
        for waits, fn, inc in P.ops[e]:
            for s, _ in waits:
                semnames.add(s)
            if inc is not None:
                semnames.add(inc[0])
    sems = {}
    for s in sorted(semnames):
        sems[s] = es.enter_context(nc.semaphore(s))
    fin = [("E" + e, P.cnt[e]) for e in ENGS if P.cnt[e] > 0 and e != "sp"] + [("D" + s, v) for s, v in P.dsem.items()]
    P.ops["sp"].append((fin, None, None))

    engmap = {"pe": "tensor", "act": "scalar", "dve": "vector", "pool": "gpsimd", "sp": "sync"}
    with nc.Block() as block:
        def make(ename):
            def body(eng):
                for waits, fn, inc in P.ops[ename]:
                    for s, v in waits:
                        eng.wait_ge(sems[s], v)
                    if fn is not None:
                        ins = fn(eng)
                        ins.then_inc(sems[inc[0]], inc[1])
            return body
        for ename in ENGS:
            getattr(block, engmap[ename])(make(ename))
    try:
        es.close()
    except AssertionError:
        pass
    return nc


_NC_CACHE = {}


def _rows_to_T(v, ncol):
    return np.ascontiguousarray(v.reshape(ncol, 128).T)


def kernel(x_prompt, x_sample, state_conv, state_lru, state_ret,
           ffn1_w_gate, ffn1_w_up, ffn1_w_down, ln1_g, ln1_b,
           w_in, conv_w, conv_b, rg_w, rg_b, ig_w, ig_b, lru_lambda,
           ret_gn_g, ret_gn_b, w_a_proj, w_b_proj, w_o, ln2_g, ln2_b,
           ffn2_w_gate, ffn2_w_up, ffn2_w_down, ln3_g, ln3_b):
    f = lambda a: np.ascontiguousarray(np.asarray(a, dtype=np.float32))
    x_prompt, x_sample = f(x_prompt), f(x_sample)
    state_conv, state_lru, state_ret = f(state_conv), f(state_lru), f(state_ret)
    if "nc" not in _NC_CACHE:
        _NC_CACHE["nc"] = build_nc()
    nc = _NC_CACHE["nc"]

    vecT = np.zeros((128, NV), np.float32)
    for i, v in enumerate([ln1_g, ln1_b, ln2_g, ln2_b, ln3_g, ln3_b]):
        vecT[:, i * 16:(i + 1) * 16] = _rows_to_T(f(v)[0], 16)
    cw = f(conv_w)[0]
    for c in range(8):
        for j in range(4):
            vecT[:, V_CW + c * 4 + j] = cw[j, c * 128:(c + 1) * 128]
    vecT[:, V_CB:V_CB + 8] = _rows_to_T(f(conv_b)[0], 8)
    vecT[:, V_RGB:V_RGB + 8] = _rows_to_T(f(rg_b)[0], 8)
    vecT[:, V_IGB:V_IGB + 8] = _rows_to_T(f(ig_b)[0], 8)
    vecT[:, V_LAM:V_LAM + 8] = _rows_to_T(f(lru_lambda)[0], 8)
    vecT[:, V_GNG:V_GNG + 8] = _rows_to_T(f(ret_gn_g)[0], 8)
    vecT[:, V_GNB:V_GNB + 8] = _rows_to_T(f(ret_gn_b)[0], 8)
    bdm = np.zeros((128, 16, 128), np.float32)
    for c in range(8):
        for gi, w in enumerate([f(rg_w)[0], f(ig_w)[0]]):
            bdm[0:64, 2 * c + gi, 0:64] = w[2 * c]
            bdm[64:128, 2 * c + gi, 64:128] = w[2 * c + 1]
    bdm = bdm.reshape(128, 16 * 128)
    ident = np.eye(128, dtype=np.float32)
    perm = np.zeros((128, 128), np.float32)
    for m in range(128):
        perm[(m + 64) % 128, m] = 1.0
    sc = np.float32(128.0 ** -0.5)
    idx = np.arange(64, dtype=np.float32)
    dmT = np.zeros((128, H, 128), np.float32)
    qdec = np.zeros((128, H, 64), np.float32)
    kdecP = np.zeros((128, H), np.float32)
    for h in range(H):
        lg = np.float32(np.log1p(-np.exp2(np.float32(-5.0 - h))))
        diff = idx[None, :] - idx[:, None]
        m = np.where(diff >= 0, np.exp(lg * np.maximum(diff, 0.0)), 0.0).astype(np.float32) * sc
        dmT[0:64, h, 0:64] = m
        dmT[64:128, h, 64:128] = m
        qdec[:, h, :] = np.exp(lg * (idx + 1.0))[None, :]
        kd = np.exp(lg * (63.0 - idx)).astype(np.float32) * sc
        kdecP[0:64, h] = kd
        kdecP[64:128, h] = kd
    inv_freq = (np.float32(10000.0) ** (-np.arange(0, 128, 2, dtype=np.float32) / np.float32(128))).astype(np.float32)
    invp = np.concatenate([inv_freq, inv_freq])
    sgn = np.concatenate([-np.ones(64, np.float32), np.ones(64, np.float32)])
    shared = {
        "vecT": vecT, "bd": bdm, "ident": ident, "perm": perm,
        "dmT": dmT.reshape(128, H * 128), "qdec": qdec.reshape(128, H * 64), "kdecP": kdecP,
        "f1g": f(ffn1_w_gate)[0], "f1u": f(ffn1_w_up)[0], "f1d": f(ffn1_w_down)[0], "w_in": f(w_in)[0],
        "w_a": f(w_a_proj)[0], "w_b": f(w_b_proj)[0], "w_o": f(w_o)[0],
        "f2g": f(ffn2_w_gate)[0], "f2u": f(ffn2_w_up)[0], "f2d": f(ffn2_w_down)[0],
    }
    in_maps = []
    for c in range(8):
        k, half = c // 2, c % 2
        xm = np.concatenate([x_prompt[k, half * 1024:(half + 1) * 1024], x_sample[4 * c:4 * c + 4].reshape(256, D)], 0)
        xp = x_prompt[k, 0:1024] if half else np.zeros((1024, D), np.float32)
        pos = np.concatenate([half * 1024 + np.arange(1024), np.tile(2048 + np.arange(64), 4), np.arange(1024)]).astype(np.float32)
        ang = (pos[None, :] * invp[:, None]).astype(np.float32)
        cs = np.stack([np.cos(ang), np.sin(ang) * sgn[:, None]], axis=1).astype(np.float32)
        stc = state_conv[0, 4 * c:4 * c + 4]
        stcT = stc.reshape(4, 3, 8, 128).transpose(3, 2, 0, 1).reshape(128, 96)
        stl = state_lru[0, 4 * c:4 * c + 4]
        stlT = stl.reshape(4, 8, 128).transpose(2, 1, 0).reshape(128, 32)
        m = dict(shared)
        m.update({
            "xmT": np.ascontiguousarray(xm.T), "xpT": np.ascontiguousarray(xp.T),
            "mask": np.full((128, 1), float(half), np.float32), "cs": np.ascontiguousarray(cs),
            "stcT": np.ascontiguousarray(stcT), "stlT": np.ascontiguousarray(stlT),
            "strt": np.ascontiguousarray(state_ret[0, 4 * c:4 * c + 4]),
        })
        in_maps.append(m)
    res = run_bass_kernel_spmd(nc, in_maps, core_ids=list(range(8)))
    R = res.results
    if os.environ.get("KDEBUG"):
        globals()["_DBG"] = [r["dbg"] for r in R]

    y_prompt = np.zeros((4, 2048, D), np.float32)
    y_sample = np.zeros((32, 64, D), np.float32)
    conv_p = np.zeros((1, 4, 3, 1024), np.float32)
    lru_p = np.zeros((1, 4, 1024), np.float32)
    ret_p = np.zeros((1, 4, H, 128, 128), np.float32)
    conv_s = np.zeros((1, 32, 3, 1024), np.float32)
    lru_s = np.zeros((1, 32, 1024), np.float32)
    ret_s = np.zeros((1, 32, H, 128, 128), np.float32)
    for c in range(8):
        k, half = c // 2, c % 2
        yT = R[c]["yT"]
        y_prompt[k, half * 1024:(half + 1) * 1024] = yT[:, 0:1024].T
        y_sample[4 * c:4 * c + 4] = yT[:, 1024:1280].T.reshape(4, 64, D)
        ho = R[c]["hoT"].reshape(128, 8, 5)
        co = R[c]["coT"].reshape(128, 8, 5, 3)
        So = R[c]["S_o"]
        hrow = ho.transpose(2, 1, 0).reshape(5, 1024)
        crow = co.transpose(2, 3, 1, 0).reshape(5, 3, 1024)
        if half == 1:
            conv_p[0, k] = crow[0]
            lru_p[0, k] = hrow[0]
            ret_p[0, k] = So[0]
        conv_s[0, 4 * c:4 * c + 4] = crow[1:5]
        lru_s[0, 4 * c:4 * c + 4] = hrow[1:5]
        ret_s[0, 4 * c:4 * c + 4] = So[1:5]
    return (y_prompt, y_sample, conv_p, lru_p, ret_p, conv_s, lru_s, ret_s)
```
